# Optimizing a Trainium2 kernel written in Bass

```python
import math
import jax, jax.numpy as jnp
from jax import lax
import numpy as np

D_MODEL = 1024
BATCH = 8
SEQ = 4096
DEPTH = 1

LRU_WIDTH = 1024
LRU_BLOCKS = 8
LRU_BLOCK = LRU_WIDTH // LRU_BLOCKS
CONV_WIDTH = 4
LRU_C = 8.0
ATT_HEADS = 8
HEAD_DIM = 64
V_DIM = 2 * HEAD_DIM
ATT_WIDTH = ATT_HEADS * V_DIM
QK_WIDTH = ATT_HEADS * 2 * HEAD_DIM
D_MIX = LRU_WIDTH + ATT_WIDTH
OFF_LRU_X = 0
OFF_LRU_G = OFF_LRU_X + LRU_WIDTH
OFF_Q = OFF_LRU_G + LRU_WIDTH
OFF_K = OFF_Q + QK_WIDTH
OFF_V = OFF_K + QK_WIDTH
OFF_ATT_G = OFF_V + ATT_WIDTH
D_IN = OFF_ATT_G + ATT_WIDTH
N_BUCKETS = 32
MAX_DISTANCE = 128
Q_BLOCK = 128
EPS = 1e-6
NEG_INF = -1e30

kernel_name = "hymba_rglru_diffattn_hybrid"


def rms_norm(x, g):
    xf = x.astype(jnp.float32)
    y = xf * lax.rsqrt(jnp.mean(xf * xf, axis=-1, keepdims=True) + EPS)
    return (y * g.astype(jnp.float32)).astype(x.dtype)


def t5_causal_buckets(rel):
    n = jnp.maximum(rel, 0)
    max_exact = N_BUCKETS // 2
    nf = jnp.maximum(n, 1).astype(jnp.float32)
    large = max_exact + (jnp.log(nf / max_exact) / math.log(MAX_DISTANCE / max_exact)
                         * (N_BUCKETS - max_exact)).astype(jnp.int32)
    large = jnp.minimum(large, N_BUCKETS - 1)
    return jnp.where(n < max_exact, n, large)


def causal_depthwise_conv(x, w, b):
    c = x.shape[-1]
    y = lax.conv_general_dilated(
        x, w[:, None, :].astype(x.dtype), window_strides=(1,),
        padding=[(CONV_WIDTH - 1, 0)], dimension_numbers=("NWC", "WIO", "NWC"),
        feature_group_count=c)
    return y + b.astype(x.dtype)


def rg_lru(x, w_r, b_r, w_i, b_i, lam):
    bsz, s, _ = x.shape
    xf = x.astype(jnp.float32)
    xh = xf.reshape(bsz, s, LRU_BLOCKS, LRU_BLOCK)
    r = jax.nn.sigmoid(jnp.einsum("bshi,hij->bshj", xh, w_r.astype(jnp.float32))
                       + b_r.astype(jnp.float32)).reshape(bsz, s, LRU_WIDTH)
    ig = jax.nn.sigmoid(jnp.einsum("bshi,hij->bshj", xh, w_i.astype(jnp.float32))
                        + b_i.astype(jnp.float32)).reshape(bsz, s, LRU_WIDTH)
    log_a = -LRU_C * r * jax.nn.softplus(-lam.astype(jnp.float32))
    a = jnp.exp(log_a)
    u = jnp.sqrt(-jnp.expm1(2.0 * log_a)) * (ig * xf)

    def combine(left, right):
        a_l, b_l = left
        a_r, b_r2 = right
        return a_l * a_r, a_r * b_l + b_r2

    _, h = lax.associative_scan(combine, (a, u), axis=1)
    return h.astype(x.dtype)


def diff_attention(q, k, v, rel_bias, lam):
    bsz, s = q.shape[0], q.shape[1]
    nb = s // Q_BLOCK
    scale = 1.0 / math.sqrt(HEAD_DIM)
    q1 = q[:, :, :, 0].transpose(0, 2, 1, 3)
    q2 = q[:, :, :, 1].transpose(0, 2, 1, 3)
    k1 = k[:, :, :, 0].transpose(0, 2, 1, 3)
    k2 = k[:, :, :, 1].transpose(0, 2, 1, 3)
    vt = v.transpose(0, 2, 1, 3)
    q1b = q1.reshape(bsz, ATT_HEADS, nb, Q_BLOCK, HEAD_DIM).transpose(2, 0, 1, 3, 4)
    q2b = q2.reshape(bsz, ATT_HEADS, nb, Q_BLOCK, HEAD_DIM).transpose(2, 0, 1, 3, 4)
    kpos = jnp.arange(s, dtype=jnp.int32)
    bias_tab = rel_bias.astype(jnp.float32)

    def block_fn(args):
        q1i, q2i, bi = args
        qpos = bi * Q_BLOCK + jnp.arange(Q_BLOCK, dtype=jnp.int32)
        rel = qpos[:, None] - kpos[None, :]
        bias = bias_tab[t5_causal_buckets(rel)].transpose(2, 0, 1)[None]
        mask = (rel >= 0)[None, None]
        s1 = jnp.einsum("bhqd,bhkd->bhqk", q1i, k1).astype(jnp.float32) * scale + bias
        s2 = jnp.einsum("bhqd,bhkd->bhqk", q2i, k2).astype(jnp.float32) * scale + bias
        p1 = jax.nn.softmax(jnp.where(mask, s1, NEG_INF), axis=-1)
        p2 = jax.nn.softmax(jnp.where(mask, s2, NEG_INF), axis=-1)
        p = (p1 - lam * p2).astype(vt.dtype)
        return jnp.einsum("bhqk,bhkv->bhqv", p, vt)

    out = lax.map(block_fn, (q1b, q2b, jnp.arange(nb, dtype=jnp.int32)))
    return out.transpose(1, 0, 3, 2, 4).reshape(bsz, s, ATT_HEADS, V_DIM)


def setup_inputs(seed: int = 0) -> dict:
    key = jax.random.key(seed)
    ks = jax.random.split(key, 20)
    f32 = jnp.float32
    nrm = lambda k, shp, sc: (jax.random.normal(k, shp, f32) * sc)
    x = jax.random.normal(ks[0], (BATCH, SEQ, D_MODEL), f32)
    norm_gain = 1.0 + nrm(ks[1], (DEPTH, D_MODEL), 0.02)
    w_in = nrm(ks[2], (DEPTH, D_MODEL, D_IN), D_MODEL ** -0.5)
    conv_w = nrm(ks[3], (DEPTH, CONV_WIDTH, LRU_WIDTH), CONV_WIDTH ** -0.5)
    conv_b = nrm(ks[4], (DEPTH, LRU_WIDTH), 0.02)
    w_rg = nrm(ks[5], (DEPTH, LRU_BLOCKS, LRU_BLOCK, LRU_BLOCK), LRU_BLOCK ** -0.5)
    b_rg = nrm(ks[6], (DEPTH, LRU_BLOCKS, LRU_BLOCK), 0.02)
    w_ig = nrm(ks[7], (DEPTH, LRU_BLOCKS, LRU_BLOCK, LRU_BLOCK), LRU_BLOCK ** -0.5)
    b_ig = nrm(ks[8], (DEPTH, LRU_BLOCKS, LRU_BLOCK), 0.02)
    a_c = jax.random.uniform(ks[9], (DEPTH, LRU_WIDTH), f32, 0.9, 0.999)
    a = a_c ** (1.0 / LRU_C)
    lru_lambda = jnp.log(a) - jnp.log1p(-a)
    q_norm_gain = 1.0 + nrm(ks[10], (DEPTH, HEAD_DIM), 0.02)
    k_norm_gain = 1.0 + nrm(ks[11], (DEPTH, HEAD_DIM), 0.02)
    lambda_q1 = nrm(ks[12], (DEPTH, HEAD_DIM), 0.1)
    lambda_k1 = nrm(ks[13], (DEPTH, HEAD_DIM), 0.1)
    lambda_q2 = nrm(ks[14], (DEPTH, HEAD_DIM), 0.1)
    lambda_k2 = nrm(ks[15], (DEPTH, HEAD_DIM), 0.1)
    subln_gain = 1.0 + nrm(ks[16], (DEPTH, V_DIM), 0.02)
    w_out = nrm(ks[17], (DEPTH, D_MIX, D_MODEL), D_MIX ** -0.5)
    rel_bias = nrm(ks[18], (N_BUCKETS, ATT_HEADS), 0.5)
    return {"x": x, "norm_gain": norm_gain, "w_in": w_in, "conv_w": conv_w,
            "conv_b": conv_b, "w_rg": w_rg, "b_rg": b_rg, "w_ig": w_ig, "b_ig": b_ig,
            "lru_lambda": lru_lambda, "q_norm_gain": q_norm_gain,
            "k_norm_gain": k_norm_gain, "lambda_q1": lambda_q1, "lambda_k1": lambda_k1,
            "lambda_q2": lambda_q2, "lambda_k2": lambda_k2, "subln_gain": subln_gain,
            "w_out": w_out, "rel_bias": rel_bias}


def reference(x, norm_gain, w_in, conv_w, conv_b, w_rg, b_rg, w_ig, b_ig, lru_lambda,
              q_norm_gain, k_norm_gain, lambda_q1, lambda_k1, lambda_q2, lambda_k2,
              subln_gain, w_out, rel_bias):
    bsz, s, _ = x.shape
    for l in range(DEPTH):
        h = rms_norm(x, norm_gain[l])
        z = jnp.einsum("bsd,de->bse", h, w_in[l])
        x_lru = z[..., OFF_LRU_X:OFF_LRU_G]
        g_lru = z[..., OFF_LRU_G:OFF_Q]
        zq = z[..., OFF_Q:OFF_K].reshape(bsz, s, ATT_HEADS, 2, HEAD_DIM)
        zk = z[..., OFF_K:OFF_V].reshape(bsz, s, ATT_HEADS, 2, HEAD_DIM)
        zv = z[..., OFF_V:OFF_ATT_G].reshape(bsz, s, ATT_HEADS, V_DIM)
        g_att = z[..., OFF_ATT_G:D_IN]

        xc = causal_depthwise_conv(x_lru, conv_w[l], conv_b[l])
        y_lru = rg_lru(xc, w_rg[l], b_rg[l], w_ig[l], b_ig[l], lru_lambda[l])
        y_lru = y_lru * jax.nn.silu(g_lru)

        q = rms_norm(zq, q_norm_gain[l])
        k = rms_norm(zk, k_norm_gain[l])
        lam_init = 0.8 - 0.6 * math.exp(-0.3 * l)
        lam = (jnp.exp(jnp.sum(lambda_q1[l].astype(jnp.float32) * lambda_k1[l].astype(jnp.float32)))
               - jnp.exp(jnp.sum(lambda_q2[l].astype(jnp.float32) * lambda_k2[l].astype(jnp.float32)))
               + lam_init)
        att = diff_attention(q, k, zv, rel_bias, lam)
        att = rms_norm(att, subln_gain[l]) * (1.0 - lam_init)
        y_att = att.reshape(bsz, s, ATT_WIDTH) * jax.nn.silu(g_att)

        mix = jnp.concatenate([y_lru, y_att], axis=-1)
        x = x + jnp.einsum("bse,ed->bsd", mix, w_out[l])
    return x
```

```python
import contextlib
import math

import numpy as np
import ml_dtypes
import concourse.bass as bass
import concourse.mybir as mybir
from concourse.bass_utils import run_bass_kernel_spmd

F32 = mybir.dt.float32
BF16 = mybir.dt.bfloat16
AF = mybir.ActivationFunctionType
ALU = mybir.AluOpType
AX = mybir.AxisListType

S = 4096
D = 1024
DIN = 6144
OFF_LX, OFF_LG, OFF_Q, OFF_K, OFF_V, OFF_AG = 0, 1024, 2048, 3072, 4096, 5120
EPS = 1e-6
LAM_INIT = 0.8 - 0.6 * math.exp(-0.3 * 0)
NCV = 80
MASKV = -30000.0
SAME_ENG_SYNC = True


class _Op:
    __slots__ = ("eng", "fn", "deps", "signal", "value", "dom", "ndma", "idx")


class Sched:
    ENGS = ("pe", "act", "dve", "pool", "sp")

    def __init__(self):
        self.q = {e: [] for e in self.ENGS}
        self.lastw = {}
        self.lastr = {}
        self.domops = {}
        self.bar = {}

    def op(self, eng, fn, reads=(), writes=(), dma=None, ndma=1):
        o = _Op()
        o.eng = eng
        o.fn = fn
        o.signal = False
        o.value = 0
        o.ndma = ndma
        o.dom = ("dma", dma) if dma is not None else eng
        deps = dict(self.bar)

        def add(d):
            if d is None:
                return
            k = d.dom
            if k not in deps or deps[k].idx < d.idx:
                deps[k] = d

        for r in reads:
            add(self.lastw.get(r))
        for r in writes:
            add(self.lastw.get(r))
            for d in self.lastr.get(r, {}).values():
                add(d)
        lst = self.domops.setdefault(o.dom, [])
        o.idx = len(lst)
        lst.append(o)
        for r in writes:
            self.lastw[r] = o
            self.lastr[r] = {}
        for r in reads:
            self.lastr.setdefault(r, {})[o.dom] = o
        o.deps = list(deps.values())
        self.q[eng].append(o)
        return o

    def barrier(self):
        self.bar = {dom: lst[-1] for dom, lst in self.domops.items() if lst}

    @staticmethod
    def _skip(o, d):
        return d.dom == o.dom and not isinstance(d.dom, tuple) and (o.eng == "pe" or not SAME_ENG_SYNC)

    def finalize(self):
        for ops in self.q.values():
            for o in ops:
                for d in o.deps:
                    if not self._skip(o, d):
                        d.signal = True
        for dom, lst in self.domops.items():
            c = 0
            for o in lst:
                if isinstance(dom, tuple):
                    c += 16 * o.ndma
                    o.value = c
                elif o.signal:
                    c += 1
                    o.value = c

    def emit(self, name, eng, sems):
        waited = {}
        for o in self.q[name]:
            for d in o.deps:
                if self._skip(o, d):
                    continue
                if waited.get(d.dom, 0) >= d.value:
                    continue
                eng.wait_ge(sems[d.dom], d.value)
                waited[d.dom] = d.value
            r = o.fn(eng)
            if isinstance(o.dom, tuple):
                for i in r:
                    i.then_inc(sems[o.dom], 16)
            elif o.signal:
                r.then_inc(sems[o.dom], 1)


def build(stop=None, dbg=False):
    nc = bass.Bass("TRN2", target_bir_lowering=False)

    def dram(name, shape, dtype, kind):
        return nc.dram_tensor(name, shape, dtype, kind=kind).ap()

    x_d = dram("x", [S, D], F32, "ExternalInput")
    win_d = dram("w_in", [D, DIN], F32, "ExternalInput")
    wout_d = dram("w_out", [2048, D], F32, "ExternalInput")
    cvec_d = dram("cvec", [128, NCV], F32, "ExternalInput")
    lamv_d = dram("lamv", [128, 256], F32, "ExternalInput")
    relb_d = dram("relb", [32, 8], F32, "ExternalInput")
    e1_d = dram("e1", [33, 384], F32, "ExternalInput")
    cst_d = dram("cst", [128, 256], F32, "ExternalInput")
    wrg_d = dram("w_rg", [128, 8, 128], F32, "ExternalInput")
    wig_d = dram("w_ig", [128, 8, 128], F32, "ExternalInput")
    out_d = dram("out", [S, D], F32, "ExternalOutput")
    mix_d = dram("mixscr", [16, 128, S], BF16, "ExternalOutput" if dbg else "Internal")
    vrow_d = dram("vrow", [8, 384], F32, "Internal")
    dbg_d = None

    SB0 = 16512
    SBTOP = 229344
    cur = [SB0]
    nid = [0]

    def sb(shape, dtype, at=None):
        nbytes = int(np.prod(shape[1:])) * (4 if dtype == F32 else 2)
        nbytes = (nbytes + 63) // 64 * 64
        if at is None:
            off = cur[0]
            cur[0] += nbytes
        else:
            off = at[0]
            at[0] += nbytes
        assert off + nbytes <= SBTOP, ("SBUF overflow", off + nbytes - SBTOP)
        nid[0] += 1
        return nc.alloc_sbuf_tensor_at("t%d" % nid[0], list(shape), dtype, offset=off)

    ps = nc.alloc_psum_tensor("ps", [128, 8, 512], F32)

    xT = sb([128, 8, S], BF16)
    cst = sb([128, 256], F32)
    ident = cst[:, 0:128]
    jmat = cst[:, 128:256]
    ones_f = sb([128, 128], F32)
    bones = sb([128, 128], BF16)
    cvec = sb([128, NCV], F32)
    lamv = sb([128, 256], F32)
    small = sb([128, 160], F32)
    bhi = sb([128, 8, 256], BF16)
    blo = sb([128, 8, 256], BF16)
    identb = sb([128, 128], BF16)
    PH0 = cur[0]
    PHTOP = (SBTOP // 64) * 64 - 8192 - 6144
    at_top = [PHTOP]
    wh0_top = sb([128, 3, 8, 128], BF16, at_top)
    wv = sb([128, 8, 512], BF16, at_top)
    wh = [wh0_top, None]

    def load_wh(h):
        srcs = [OFF_Q + h * 128, OFF_K + h * 128, OFF_AG + h * 128]
        op("pool", lambda e, h=h, srcs=srcs: [
            e.dma_start(out=wh[h % 2][:, i, :, :], in_=win_d[:, srcs[i]:srcs[i] + 128].rearrange("(k p) n -> p k n", p=128)) for i in range(3)],
           writes=[("wh", h % 2)], dma="wh%d" % (h % 2), ndma=3)

    def load_wv(g):
        op("pool", lambda e, g=g: [e.dma_start(out=wv[:], in_=win_d[:, OFF_V + g * 512:OFF_V + (g + 1) * 512].rearrange("(k p) n -> p k n", p=128))],
           writes=["wv"], dma="wv")

    CV_NG, CV_CB, CV_LL, CV_BR, CV_BI, CV_CW, CV_QG, CV_KG, CV_SG = 0, 8, 16, 24, 32, 40, 72, 73, 74
    SM_SS, SM_RSTD, SM_HBR, SM_HBI, SM_CC, SM_HC, SM_GQ8, SM_NLAM, SM_T0, SM_LS = 0, 32, 64, 72, 80, 88, 96, 97, 100, 120

    sch = Sched()
    op = sch.op

    def col(t, c):
        return t[:, c:c + 1]

    op("sp", lambda e: [e.dma_start(out=cst[:], in_=cst_d)], writes=["cst"], dma="cst")
    op("sp", lambda e: [e.dma_start(out=cvec[:], in_=cvec_d)], writes=["cvec"], dma="cvec")
    op("sp", lambda e: [e.dma_start(out=lamv[:], in_=lamv_d)], writes=["lamv"], dma="lamv")
    op("dve", lambda e: e.memset(ones_f[:], 1.0), writes=["ones_f"])
    op("dve", lambda e: e.memset(bones[:], 0.0), writes=["bones"])
    op("dve", lambda e: e.memset(bones[0:64, 0:64], 1.0), writes=["bones"])
    op("dve", lambda e: e.memset(bones[64:128, 64:128], 1.0), writes=["bones"])

    at = [PH0]
    rbaug = sb([33, 8], F32, at)
    e1s = sb([33, 384], F32, at)
    vsb = sb([8, 384], F32, at)
    hk8 = sb([128, 8, 256], F32, at)
    lprod = sb([128, 128], F32, at)
    biasT = sb([128, 8, 256], F32, at)
    btmp = sb([128, 8, 256], F32, at)
    PHA = at[0]

    op("dve", lambda e: e.memset(rbaug[32:33, :], MASKV), writes=["rbaug"])
    op("sp", lambda e: [e.dma_start(out=rbaug[0:32, :], in_=relb_d)], writes=["rbaug"], dma="rbaug")
    op("sp", lambda e: [e.dma_start(out=e1s[:], in_=e1_d)], writes=["e1s"], dma="e1s")
    op("pe", lambda e: e.matmul(ps[0:8, 0, 0:384], rbaug[:, :], e1s[:, :], start=True, stop=True),
       reads=["rbaug", "e1s"], writes=[("ps", 0)])
    op("dve", lambda e: e.tensor_copy(out=vsb[:], in_=ps[0:8, 0, 0:384]), reads=[("ps", 0)], writes=["vsb"])
    op("sp", lambda e: [e.dma_start(out=vrow_d, in_=vsb[:])], reads=["vsb"], writes=["vrow_d"], dma="vrow")
    op("sp", lambda e: [e.dma_start(out=hk8[:], in_=bass.AP(vrow_d.tensor, 0, [[1, 128], [384, 8], [1, 256]]))],
       reads=["vrow_d"], writes=["hk8"], dma="hk8")
    for h in range(8):
        b = h % 4
        op("pe", lambda e, h=h, b=b: e.matmul(ps[:, b, 0:256], jmat, hk8[:, h, :], start=True, stop=True),
           reads=["hk8", "cst"], writes=[("ps", b)])
        op("dve", lambda e, h=h, b=b: e.tensor_copy(out=biasT[:, h, :], in_=ps[:, b, 0:256]),
           reads=[("ps", b)], writes=["biasT"])

    op("dve", lambda e: e.tensor_copy(out=bhi[:], in_=biasT[:]), reads=["biasT"], writes=["bhi"])
    op("dve", lambda e: e.tensor_tensor(out=btmp[:], in0=biasT[:], in1=bhi[:], op=ALU.subtract), reads=["biasT", "bhi"], writes=["btmp"])
    op("dve", lambda e: e.tensor_copy(out=blo[:], in_=btmp[:]), reads=["btmp"], writes=["blo"])
    op("dve", lambda e: e.tensor_copy(out=identb[:], in_=ident), reads=["cst"], writes=["identb"])
    op("dve", lambda e: e.tensor_tensor(out=lprod[:, 0:64], in0=lamv[:, 0:64], in1=lamv[:, 64:128], op=ALU.mult),
       reads=["lamv"], writes=["lprod"])
    op("dve", lambda e: e.tensor_tensor(out=lprod[:, 64:128], in0=lamv[:, 128:192], in1=lamv[:, 192:256], op=ALU.mult),
       reads=["lamv"], writes=["lprod"])
    op("dve", lambda e: e.tensor_reduce(out=small[:, SM_LS:SM_LS + 2], in_=lprod[:].rearrange("p (a b) -> p a b", a=2),
                                        axis=AX.X, op=ALU.add), reads=["lprod"], writes=["sm_ls"])
    op("act", lambda e: e.activation(out=small[:, SM_LS + 2:SM_LS + 4], in_=small[:, SM_LS:SM_LS + 2], func=AF.Exp),
       reads=["sm_ls"], writes=["sm_le"])
    op("dve", lambda e: e.tensor_tensor(out=col(small, SM_LS + 4), in0=col(small, SM_LS + 2), in1=col(small, SM_LS + 3),
                                        op=ALU.subtract), reads=["sm_le"], writes=["sm_l"])
    op("dve", lambda e: e.tensor_scalar(out=col(small, SM_NLAM), in0=col(small, SM_LS + 4), scalar1=LAM_INIT, scalar2=-1.0,
                                        op0=ALU.add, op1=ALU.mult), reads=["sm_l"], writes=["nlam"])
    op("act", lambda e: e.activation(out=small[:, SM_T0:SM_T0 + 8], in_=cvec[:, CV_LL:CV_LL + 8], func=AF.Exp, scale=-1.0),
       reads=["cvec"], writes=["sm_t0"])
    op("act", lambda e: e.activation(out=small[:, SM_T0 + 8:SM_T0 + 16], in_=small[:, SM_T0:SM_T0 + 8], func=AF.Ln, bias=1.0),
       reads=["sm_t0"], writes=["sm_t1"])
    op("dve", lambda e: e.tensor_scalar(out=small[:, SM_CC:SM_CC + 8], in0=small[:, SM_T0 + 8:SM_T0 + 16], scalar1=-8.0,
                                        scalar2=None, op0=ALU.mult), reads=["sm_t1"], writes=["lruc"])
    op("dve", lambda e: e.tensor_scalar(out=small[:, SM_HC:SM_HC + 8], in0=small[:, SM_T0 + 8:SM_T0 + 16], scalar1=-4.0,
                                        scalar2=None, op0=ALU.mult), reads=["sm_t1"], writes=["lruc"])
    op("dve", lambda e: e.tensor_scalar(out=small[:, SM_HBR:SM_HBR + 16], in0=cvec[:, CV_BR:CV_BR + 16], scalar1=0.5,
                                        scalar2=None, op0=ALU.mult), reads=["cvec"], writes=["lruc"])
    op("dve", lambda e: e.tensor_scalar(out=col(small, SM_GQ8), in0=col(cvec, CV_QG), scalar1=0.125, scalar2=None,
                                        op0=ALU.mult), reads=["cvec"], writes=["gq8"])

    at = [PHA]
    xa = [sb([128, 4, D], F32, at) for _ in range(3)]
    sqj = sb([128, D], BF16, at)
    dg = [sb([128, 128], F32, at) for _ in range(2)]
    lnt0 = sb([128, 32], F32, at)

    for tg in range(8):
        buf = xa[tg % 3]
        bk = [("xa", tg % 3, j) for j in range(4)]
        op("sp", lambda e, tg=tg, buf=buf: [e.dma_start(
            out=buf[:], in_=x_d[tg * 512:(tg + 1) * 512, :].rearrange("(j p) d -> p j d", p=128))],
           writes=bk, dma="xa%d" % (tg % 3))
        for j in range(4):
            tb = tg * 4 + j
            op("act", lambda e, buf=buf, j=j, tb=tb: e.activation(out=sqj[:], in_=buf[:, j, :], func=AF.Square,
                                                                  accum_out=col(small, SM_SS + tb)),
               reads=[bk[j]], writes=["sqj", ("ss", tg)])
        op("act", lambda e, tg=tg: e.activation(out=lnt0[:, tg * 4:tg * 4 + 4], in_=small[:, SM_SS + tg * 4:SM_SS + tg * 4 + 4],
                                               func=AF.Ln, scale=1.0 / D, bias=EPS), reads=[("ss", tg)], writes=[("lnt0", tg)])
        op("act", lambda e, tg=tg: e.activation(out=small[:, SM_RSTD + tg * 4:SM_RSTD + tg * 4 + 4], in_=lnt0[:, tg * 4:tg * 4 + 4],
                                               func=AF.Exp, scale=-0.5), reads=[("lnt0", tg)], writes=[("rstd", tg)])
        for j in range(4):
            tb = tg * 4 + j
            if j < 3:
                op("act", lambda e, buf=buf, j=j, tb=tb: e.activation(out=buf[:, j, :], in_=buf[:, j, :], func=AF.Copy,
                                                                      scale=col(small, SM_RSTD + tb)),
                   reads=[bk[j], ("rstd", tg)], writes=[bk[j]])
            else:
                op("dve", lambda e, buf=buf, j=j, tb=tb: e.tensor_scalar(out=buf[:, j, :], in0=buf[:, j, :], scalar1=col(small, SM_RSTD + tb),
                                                                         scalar2=1.0, op0=ALU.mult, op1=ALU.mult),
                   reads=[bk[j], ("rstd", tg)], writes=[bk[j]])
        for c in range(8):
            def tr(e, buf=buf, c=c):
                r = None
                for j in range(4):
                    r = e.transpose(ps[:, c, j * 128:(j + 1) * 128], buf[:, j, c * 128:(c + 1) * 128], ident)
                return r
            op("pe", tr, reads=bk + ["cst"], writes=[("ps", c)])
            op("dve", lambda e, c=c, tg=tg: e.tensor_scalar(out=xT[:, c, tg * 512:(tg + 1) * 512], in0=ps[:, c, :],
                                                            scalar1=col(cvec, CV_NG + c), scalar2=None, op0=ALU.mult),
               reads=[("ps", c), "cvec"], writes=[("xT", c, tg)])
    sch.barrier()

    if stop == "A":
        return _finish(nc, sch, dbg_d, None, ps, None)

    def proj_fm(wt, widx, tc, bank, wres):
        def f(e):
            r = None
            for k in range(8):
                r = e.matmul(ps[:, bank, :], wt[:, widx, k, :], xT[:, k, tc * 512:(tc + 1) * 512], start=(k == 0), stop=(k == 7))
            return r
        op("pe", f, reads=[wres] + [("xT", k, tc) for k in range(8)], writes=[("ps", bank)])

    bankc = [0]

    def nbank(n=8, lo=0):
        b = lo + bankc[0] % n
        bankc[0] += 1
        return b

    TL = 1024
    NSET = 2
    NQ = TL // 512
    NS = S // TL
    at = [PH0]
    wl = [sb([128, 2, 8, 128], BF16, at) for _ in range(2)]
    wrg = sb([128, 8, 128], BF16, at)
    wig = sb([128, 8, 128], BF16, at)
    dgw = [sb([128, 4, 128], BF16, at) for _ in range(2)]
    hcar = sb([128, 8], F32, at)

    def lset():
        return dict(xlb=sb([128, TL + 16], BF16, at), xlb2=sb([128, TL + 16], BF16, at), gl=sb([128, TL], F32, at), xc=sb([128, TL], F32, at),
                    xcb=sb([128, TL], BF16, at), rr=sb([128, TL], F32, at), ti=sb([128, TL], F32, at),
                    aa=sb([128, TL], F32, at), a2=sb([128, TL], F32, at), uu=sb([128, TL], F32, at),
                    hh=sb([128, TL], F32, at), tg=sb([128, TL], F32, at), ylb=sb([128, TL], BF16, at))
    LS = [lset() for _ in range(NSET)]
    assert at[0] <= PHTOP, (at[0], PHTOP)

    op("pool", lambda e: [e.dma_start(out=wrg[:], in_=wrg_d)], writes=["wrg"], dma="wrg")
    op("pool", lambda e: [e.dma_start(out=wig[:], in_=wig_d)], writes=["wig"], dma="wig")

    def load_wl(c):
        w = wl[c % 2]
        op("pool", lambda e, c=c, w=w: [
            e.dma_start(out=w[:, 0, :, :], in_=win_d[:, OFF_LX + c * 128:OFF_LX + (c + 1) * 128].rearrange("(k p) n -> p k n", p=128)),
            e.dma_start(out=w[:, 1, :, :], in_=win_d[:, OFF_LG + c * 128:OFF_LG + (c + 1) * 128].rearrange("(k p) n -> p k n", p=128)),
        ], writes=[("wl", c % 2)], dma="wl%d" % (c % 2), ndma=2)

    def lru_front(u):
        c, s_ = divmod(u, NS)
        p = u % NSET
        po = (u - 1) % NSET
        B = LS[p]
        w = wl[c % 2]
        wres = ("wl", c % 2)
        R = lambda n: ("L", n, p)
        if s_ == 0:
            if c + 1 < 8:
                load_wl(c + 1)
            for j in range(4):
                op("dve", lambda e, j=j, c=c: e.tensor_scalar(out=dgw[c % 2][:, j, :], in0=ident, scalar1=col(cvec, CV_CW + c * 4 + j),
                                                              scalar2=None, op0=ALU.mult), reads=["cst", "cvec"], writes=[("dgw", c % 2)])
            op("dve", lambda e: e.memset(B["xlb"][:, 4:8], 0.0), writes=[R("xlb")])
        else:
            Bo = LS[po]
            op("dve", lambda e: e.tensor_copy(out=B["xlb"][:, 4:8], in_=Bo["xlb"][:, TL + 4:TL + 8]), reads=[("L", "xlb", po)], writes=[R("xlb")])
        for q in range(NQ):
            tc = s_ * NQ + q
            sl = slice(q * 512, (q + 1) * 512)
            b1 = nbank()
            proj_fm(w, 0, tc, b1, wres)
            op("dve", lambda e, q=q, b1=b1: e.tensor_copy(out=B["xlb"][:, 8 + q * 512:8 + (q + 1) * 512], in_=ps[:, b1, :]),
               reads=[("ps", b1)], writes=[R("xlb")])
            b2 = nbank()
            proj_fm(w, 1, tc, b2, wres)
            op("act", lambda e, sl=sl, b2=b2: e.activation(out=B["gl"][:, sl], in_=ps[:, b2, :], func=AF.Copy),
               reads=[("ps", b2)], writes=[R("gl")])
        op("dve", lambda e: e.tensor_copy(out=B["xlb2"][:, 5:TL + 9], in_=B["xlb"][:, 4:TL + 8]), reads=[R("xlb")], writes=[R("xlb2")])
        taps = [("xlb2", 6), ("xlb", 6), ("xlb2", 8), ("xlb", 8)]
        for q in range(NQ):
            sl = slice(q * 512, (q + 1) * 512)
            b3 = nbank()

            def fcv(e, q=q, b3=b3, c=c):
                r = None
                for j in range(4):
                    nm, o_ = taps[j]
                    r = e.matmul(ps[:, b3, :], dgw[c % 2][:, j, :], B[nm][:, o_ + q * 512:o_ + (q + 1) * 512],
                                 start=(j == 0), stop=(j == 3))
                return r
            op("pe", fcv, reads=[R("xlb"), R("xlb2"), ("dgw", c % 2)], writes=[("ps", b3)])
            op("dve", lambda e, sl=sl, b3=b3, c=c: e.tensor_scalar(out=B["xcb"][:, sl], in0=ps[:, b3, :], scalar1=col(cvec, CV_CB + c),
                                                                   scalar2=None, op0=ALU.add), reads=[("ps", b3), "cvec"], writes=[R("xcb")])
        for q in range(NQ):
            sl = slice(q * 512, (q + 1) * 512)
            b1 = nbank()
            op("pe", lambda e, c=c, sl=sl, b1=b1: e.matmul(ps[:, b1, :], wrg[:, c, :], B["xcb"][:, sl], start=True, stop=True),
               reads=["wrg", R("xcb")], writes=[("ps", b1)])
            op("act", lambda e, c=c, sl=sl, b1=b1: e.activation(out=B["rr"][:, sl], in_=ps[:, b1, :], func=AF.Tanh, scale=0.5,
                                                                bias=col(small, SM_HBR + c)), reads=[("ps", b1), "lruc"], writes=[R("rr")])
            b2 = nbank()
            op("pe", lambda e, c=c, sl=sl, b2=b2: e.matmul(ps[:, b2, :], wig[:, c, :], B["xcb"][:, sl], start=True, stop=True),
               reads=["wig", R("xcb")], writes=[("ps", b2)])
            op("act", lambda e, c=c, sl=sl, b2=b2: e.activation(out=B["ti"][:, sl], in_=ps[:, b2, :], func=AF.Tanh, scale=0.5,
                                                                bias=col(small, SM_HBI + c)), reads=[("ps", b2), "lruc"], writes=[R("ti")])
        op("act", lambda e: e.activation(out=B["tg"][:], in_=B["gl"][:], func=AF.Silu), reads=[R("gl")], writes=[R("tg")])
        op("act", lambda e, c=c: e.activation(out=B["aa"][:], in_=B["rr"][:], func=AF.Exp, scale=col(small, SM_HC + c), bias=col(small, SM_HC + c)),
           reads=[R("rr"), "lruc"], writes=[R("aa")])
        op("act", lambda e, c=c: e.activation(out=B["a2"][:], in_=B["rr"][:], func=AF.Exp, scale=col(small, SM_CC + c), bias=col(small, SM_CC + c)),
           reads=[R("rr"), "lruc"], writes=[R("a2")])
        op("act", lambda e: e.activation(out=B["a2"][:], in_=B["a2"][:], func=AF.Ln, scale=-1.0, bias=1.0), reads=[R("a2")], writes=[R("a2")])
        op("act", lambda e: e.activation(out=B["a2"][:], in_=B["a2"][:], func=AF.Exp, scale=0.5, bias=math.log(0.5)), reads=[R("a2")], writes=[R("a2")])

    def lru_back(u):
        c, s_ = divmod(u, NS)
        p = u % NSET
        B = LS[p]
        R = lambda n: ("L", n, p)
        t0 = s_ * TL
        op("dve", lambda e: e.scalar_tensor_tensor(out=B["uu"][:], in0=B["ti"][:], scalar=1.0, in1=B["xcb"][:], op0=ALU.add, op1=ALU.mult),
           reads=[R("ti"), R("xcb")], writes=[R("uu")])
        op("dve", lambda e: e.tensor_tensor(out=B["uu"][:], in0=B["uu"][:], in1=B["a2"][:], op=ALU.mult), reads=[R("uu"), R("a2")], writes=[R("uu")])
        if s_ == 0:
            op("dve", lambda e: e.tensor_tensor_scan(out=B["hh"][:], data0=B["aa"][:], data1=B["uu"][:], initial=0.0, op0=ALU.mult, op1=ALU.add),
               reads=[R("aa"), R("uu")], writes=[R("hh")])
        else:
            op("dve", lambda e: e.tensor_tensor_scan(out=B["hh"][:], data0=B["aa"][:], data1=B["uu"][:], initial=hcar[:, 0:1], op0=ALU.mult,
                                                     op1=ALU.add), reads=[R("aa"), R("uu"), "hcar"], writes=[R("hh")])
        if s_ + 1 < NS:
            op("dve", lambda e: e.tensor_copy(out=hcar[:, 0:1], in_=B["hh"][:, TL - 1:TL]), reads=[R("hh")], writes=["hcar"])
        op("dve", lambda e: e.tensor_tensor(out=B["ylb"][:], in0=B["hh"][:], in1=B["tg"][:], op=ALU.mult),
           reads=[R("hh"), R("tg")], writes=[R("ylb")])
        op("sp", lambda e, c=c, t0=t0: [e.dma_start(out=mix_d[c, :, t0:t0 + TL], in_=B["ylb"][:])],
           reads=[R("ylb")], writes=[("mixd", c)], dma="ylb%d" % p)

    load_wl(0)
    NU = 8 * NS
    for u in range(NU + NSET - 1):
        if u < NU:
            lru_front(u)
        if u >= NSET - 1:
            lru_back(u - (NSET - 1))
    load_wh(0)
    load_wv(0)
    sch.barrier()

    if stop == "B":
        return _finish(nc, sch, dbg_d, None, ps, None)

    at = [PH0]
    Vc = sb([128, 32, 4, 130], BF16, at)
    qT = sb([128, S], BF16, at)
    kT = sb([128, S], BF16, at)
    sgT = sb([128, S], BF16, at)
    wh[1] = sb([128, 3, 8, 128], BF16, at)
    NZ = 4
    NB_IP = 7
    zsq = [sb([128, 512], BF16, at) for _ in range(NZ)]
    rq = [sb([128, 512], F32, at) for _ in range(NZ)]
    NPT = 4
    pt = [sb([128, 2, 512], BF16, at) for _ in range(NPT)]
    accs2 = [sb([128, 8 * 129], F32, at) for _ in range(2)]
    rden2 = [sb([128, 8], F32, at) for _ in range(2)]
    ot2 = [sb([128, 4, 128], F32, at) for _ in range(2)]
    junk = sb([128, 128], F32, at)
    ssq2_ = [sb([128, 8], F32, at) for _ in range(2)]
    yn2 = [sb([128, 4, 128], BF16, at) for _ in range(2)]
    ps7b = ps[:, 7, :].bitcast(BF16)
    yat = [sb([128, 512], BF16, at) for _ in range(2)]
    assert at[0] <= PHTOP, (at[0], PHTOP)

    def _addr(t):
        return t[:].tensor.manual_sbuf_range[0]
    assert _addr(zsq[0]) == _addr(wh[1]) + 6144 and _addr(rq[3]) == _addr(rq[0]) + 3 * 2048 and _addr(pt[3]) == _addr(pt[0]) + 3 * 2048
    wo_q = [nc.alloc_sbuf_tensor_at("woq%d" % i, [128, 4, D], BF16, offset=a_)
            for i, a_ in enumerate((_addr(wh[1]), _addr(rq[0]), _addr(wv), _addr(pt[0])))]
    wo_res = [[("wh", 1), ("zsq", 0), ("zsq", 1)], [("rq", i) for i in range(NZ)], ["wv"], [("pt", i) for i in range(NPT)]]

    def load_wo(wq):
        op("pool", lambda e: [e.dma_start(out=wo_q[wq][:], in_=wout_d[512 * wq:512 * (wq + 1), :].rearrange("(k p) n -> p k n", p=128))],
           writes=[("wo", wq)] + wo_res[wq], dma="wo%d" % wq)

    op("dve", lambda e: e.memset(Vc[:, :, :, 128:130], 1.0), writes=["Vc"])

    zi = [0]
    carry = []
    yidx = 0
    for h in range(8):
        if h + 1 < 8:
            load_wh(h + 1)
        w = wh[h % 2]
        wres = ("wh", h % 2)
        vgroups = []
        if h == 1:
            load_wv(1)
        if h % 4 == 0:

            def vgroup(tb):
                bk_ = nbank(NB_IP)

                def fv(e):
                    r = None
                    for k in range(8):
                        r = e.matmul(ps[:, bk_, :], xT[:, k, tb * 128:(tb + 1) * 128], wv[:, k, :], start=(k == 0), stop=(k == 7))
                    return r
                op("pe", fv, reads=["wv"] + [("xT", k, tb // 4) for k in range(8)], writes=[("ps", bk_)])
                op("dve", lambda e: e.tensor_copy(out=Vc[:, tb, :, 0:128], in_=ps[:, bk_, :].rearrange("p (a b) -> p a b", a=4)),
                   reads=[("ps", bk_)], writes=["Vc"])
            for tb_ in range(32):
                vgroup(tb_)
        units = [(which, tc) for which in (0, 1) for tc in range(8)]
        st = {}

        def ip1(which, tc):
            i2 = zi[0] % NZ
            zi[0] += 1
            b1 = nbank(NB_IP)
            st[(which, tc)] = (i2, b1)
            proj_fm(w, which, tc, b1, wres)
            op("act", lambda e: e.activation(out=zsq[i2][:], in_=ps[:, b1, :], func=AF.Square),
               reads=[("ps", b1)], writes=[("zsq", i2)])

        def ip2(which, tc):
            i2, b1 = st[(which, tc)]
            dst = qT if which == 0 else kT
            b2 = nbank(NB_IP)
            op("pe", lambda e: e.matmul(ps[:, b2, :], bones[:], zsq[i2][:], start=True, stop=True),
               reads=["bones", ("zsq", i2)], writes=[("ps", b2)])
            op("act", lambda e: e.activation(out=rq[i2][:], in_=ps[:, b2, :], func=AF.Ln, scale=1.0 / 64, bias=EPS),
               reads=[("ps", b2)], writes=[("rq", i2)])
            op("act", lambda e: e.activation(out=rq[i2][:], in_=rq[i2][:], func=AF.Exp, scale=-0.5),
               reads=[("rq", i2)], writes=[("rq", i2)])
            gsc = col(small, SM_GQ8) if which == 0 else col(cvec, CV_KG)
            op("dve", lambda e: e.scalar_tensor_tensor(
                out=dst[:, tc * 512:(tc + 1) * 512], in0=ps[:, b1, :], scalar=gsc, in1=rq[i2][:], op0=ALU.mult, op1=ALU.mult),
               reads=[("ps", b1), ("rq", i2), "gq8", "cvec"], writes=[("qk", which, tc)])

        for ui in range(len(units) + 1):
            if ui < len(units):
                ip1(*units[ui])
            if ui >= 1:
                ip2(*units[ui - 1])
            if ui >= 3 and carry:
                carry.pop(0)()
        while carry:
            carry.pop(0)()
        while vgroups:
            vgroup(vgroups.pop(0))
        for tc in range(8):
            i2 = zi[0] % NZ
            zi[0] += 1
            b1 = nbank(NB_IP)
            proj_fm(w, 2, tc, b1, wres)
            op("act", lambda e, tc=tc, b1=b1: e.activation(out=sgT[:, tc * 512:(tc + 1) * 512], in_=ps[:, b1, :], func=AF.Silu),
               reads=[("ps", b1)], writes=[("sg", tc)])
        if h == 7:
            for wq_ in range(3):
                load_wo(wq_)
        hv = h % 4
        QORDER = (0, 1, 2, 3, 4, 5, 6, 7)
        steps = [(qc, kb) for qc in QORDER for kb in range(4 * qc + 4)]
        nst = len(steps)
        deferred = []

        def s_qk(i, h=h):
            qc, kb = steps[i]
            j0 = max(0, kb - 4 * qc)
            c0 = j0 * 128
            sb_ = (i % 2) * 2
            sres = [("ps", sb_), ("ps", sb_ + 1)]

            near = kb >= 4 * qc - 1
            if near:
                if kb == 4 * qc - 1:
                    cc0, wdt, bc0 = 0, 128, 128
                else:
                    cc0, wdt, bc0 = c0, min(256, 512 - c0), 0

            def fqk(e):
                e.matmul(ps[:, sb_, c0:512], kT[0:64, kb * 128:(kb + 1) * 128], qT[0:64, qc * 512 + c0:(qc + 1) * 512],
                         start=True, stop=not near, skip_group_check=True)
                r = e.matmul(ps[:, sb_ + 1, c0:512], kT[64:128, kb * 128:(kb + 1) * 128],
                             qT[64:128, qc * 512 + c0:(qc + 1) * 512], start=True, stop=not near, skip_group_check=True)
                if near:
                    for comp in range(2):
                        e.matmul(ps[:, sb_ + comp, cc0:cc0 + wdt], identb[:], bhi[:, h, bc0:bc0 + wdt], start=False, stop=False,
                                 skip_group_check=True)
                        r = e.matmul(ps[:, sb_ + comp, cc0:cc0 + wdt], identb[:], blo[:, h, bc0:bc0 + wdt], start=False, stop=True,
                                     skip_group_check=True)
                return r
            op("pe", fqk, reads=[("qk", 0, qc), ("qk", 1, kb // 4), "bhi", "blo", "identb"], writes=sres)
            p_ = pt[i % NPT]
            op("act", lambda e: e.activation(out=p_[:, :, c0:512], in_=ps[:, sb_:sb_ + 2, c0:512], func=AF.Exp),
               reads=sres, writes=[("pt", i % NPT)])

        def s_pv(i, hv=hv):
            qc, kb = steps[i]
            j0 = max(0, kb - 4 * qc)
            p_ = pt[i % NPT]

            def fpv(e):
                r = None
                for j in range(j0, 4):
                    for comp in range(2):
                        a = j * 2 + comp
                        bank = 4 + a // 3
                        o0 = (a % 3) * 129
                        r = e.matmul(ps[:, bank, o0:o0 + 129], p_[:, comp, j * 128:(j + 1) * 128], Vc[:, kb, hv, 0:129],
                                     start=(kb == 0 and a % 3 == 0), stop=(kb == 4 * qc + j), skip_group_check=True)
                return r
            touched = sorted({4 + (j * 2 + comp) // 3 for j in range(j0, 4) for comp in range(2)})
            op("pe", fpv, reads=[("pt", i % NPT), "Vc"], writes=[("ps", b_) for b_ in touched])

        def acc_copy(bnk, ts_):
            accs = accs2[ts_]
            n = 3 if bnk < 2 else 2
            op("dve", lambda e: e.tensor_copy(out=accs[:, bnk * 387:bnk * 387 + n * 129], in_=ps[:, 4 + bnk, 0:n * 129]),
               reads=[("ps", 4 + bnk)], writes=[("accs", ts_)])

        def tail1(qc, ts_):
            accs, rden, ot, ssq = accs2[ts_], rden2[ts_], ot2[ts_], ssq2_[ts_]
            av = accs[:].rearrange("p (a b) -> p a b", a=8)
            rv = rden[:].rearrange("p (j c) -> p j c", c=2)
            op("dve", lambda e: e.reciprocal(out=rden[:], in_=av[:, :, 128]), reads=[("accs", ts_)], writes=[("rden", ts_)])
            op("dve", lambda e: e.tensor_scalar(out=rv[:, :, 1], in0=rv[:, :, 1], scalar1=col(small, SM_NLAM), scalar2=None,
                                                op0=ALU.mult), reads=[("rden", ts_), "nlam"], writes=[("rden", ts_)])
            for j in range(4):
                op("dve", lambda e, j=j: e.tensor_scalar(out=ot[:, j, :], in0=av[:, 2 * j, 0:128], scalar1=rden[:, 2 * j:2 * j + 1],
                                                         scalar2=None, op0=ALU.mult), reads=[("accs", ts_), ("rden", ts_)], writes=[("ot", ts_, j)])
                op("dve", lambda e, j=j: e.scalar_tensor_tensor(out=ot[:, j, :], in0=av[:, 2 * j + 1, 0:128],
                                                                scalar=rden[:, 2 * j + 1:2 * j + 2], in1=ot[:, j, :],
                                                                op0=ALU.mult, op1=ALU.add),
                   reads=[("accs", ts_), ("rden", ts_), ("ot", ts_, j)], writes=[("ot", ts_, j)])
                op("dve", lambda e, j=j: e.scalar_tensor_tensor(out=junk[:], in0=ot[:, j, :], scalar=1.0, in1=ot[:, j, :],
                                                                op0=ALU.mult, op1=ALU.mult, accum_out=ssq[:, j:j + 1]),
                   reads=[("ot", ts_, j)], writes=["junk", ("ssq", ts_, j)])

        def tail2(qc, ts_, h=h):
            ot, ssq, yn = ot2[ts_], ssq2_[ts_], yn2[ts_]
            op("act", lambda e: e.activation(out=ssq[:, 4:8], in_=ssq[:, 0:4], func=AF.Ln, scale=1.0 / 128, bias=EPS),
               reads=[("ssq", ts_, j) for j in range(4)], writes=[("ssqr", ts_)])
            op("act", lambda e: e.activation(out=ssq[:, 4:8], in_=ssq[:, 4:8], func=AF.Exp, scale=-0.5),
               reads=[("ssqr", ts_)], writes=[("ssqr", ts_)])
            for j in range(4):
                op("dve", lambda e, j=j: e.tensor_scalar(out=yn[:, j, :], in0=ot[:, j, :], scalar1=ssq[:, 4 + j:5 + j],
                                                         scalar2=(1.0 - LAM_INIT), op0=ALU.mult, op1=ALU.mult),
                   reads=[("ot", ts_, j), ("ssqr", ts_)], writes=[("yn", ts_)])

        def tail3(qc, ts_, h=h):
            nonlocal yidx
            yn = yn2[ts_]

            def ftr(e):
                r = None
                for j in range(4):
                    r = e.transpose(ps7b[:, j * 128:(j + 1) * 128], yn[:, j, :], identb[:])
                return r
            op("pe", ftr, reads=[("yn", ts_), "identb"], writes=[("ps", 7)])
            ya = yat[yidx % 2]
            yres = ("yat", yidx % 2)
            op("dve", lambda e: e.scalar_tensor_tensor(out=ya[:], in0=ps7b[:, 0:512], scalar=col(cvec, CV_SG),
                                                       in1=sgT[:, qc * 512:(qc + 1) * 512], op0=ALU.mult, op1=ALU.mult),
               reads=[("ps", 7), "cvec", ("sg", qc)], writes=[yres])
            op("sp", lambda e: [e.dma_start(out=mix_d[8 + h, :, qc * 512:(qc + 1) * 512], in_=ya[:])],
               reads=[yres], writes=[("mixd", 8 + h)], dma="yat%d" % (yidx % 2))
            yidx += 1

        TD = 5
        def la_of(s_):
            qc_, kb_ = steps[s_]
            return 3 if (kb_ == 0 and s_ > 0) else 2
        pvn = 0
        i = 0
        while pvn < nst:
            if i < nst:
                s_qk(i)
            for dd in [d_ for d_ in deferred if d_[0] <= i]:
                dd[1]()
                deferred.remove(dd)
            while pvn < nst and (pvn <= i - la_of(pvn) or i >= nst + 3):
                s_pv(pvn)
                qc_, kb_ = steps[pvn]
                tsi = QORDER.index(qc_) % 2
                if kb_ >= 4 * qc_ + 1:
                    acc_copy(kb_ - 4 * qc_ - 1, tsi)
                if kb_ == 4 * qc_ + 3:
                    tail1(qc_, tsi)
                    deferred.append((i + TD, lambda qc_=qc_, tsi=tsi, f_=tail2: f_(qc_, tsi)))
                    deferred.append((i + TD + 4, lambda qc_=qc_, tsi=tsi, f_=tail3: f_(qc_, tsi)))
                pvn += 1
            i += 1
        carry[:] = [dd[1] for dd in deferred]
        if h == 7:
            load_wo(3)
            for f_ in carry:
                f_()
    sch.barrier()

    if stop == "C":
        return _finish(nc, sch, dbg_d, None, ps, None)

    at = [PH0]
    mixc = [sb([128, 16, 512], BF16, at) for _ in range(2)]
    xres = [sb([128, D], F32, at) for _ in range(2)]
    ores = [sb([128, D], F32, at) for _ in range(2)]
    assert at[0] <= _addr(wh[1]), (at[0], _addr(wh[1]))
    last_stores = []
    for tc in range(8):
        m = mixc[tc % 2]
        mres = ("mixc", tc % 2)
        op("sp", lambda e, m=m, tc=tc: [e.dma_start(out=m[:], in_=mix_d[:, :, tc * 512:(tc + 1) * 512].rearrange("e p t -> p e t"))],
           reads=[("mixd", i) for i in range(16)], writes=[mres], dma="mixc%d" % (tc % 2))
        for j in range(4):
            tb = tc * 4 + j
            xr = xres[tb % 2]
            orr = ores[tb % 2]
            op("sp", lambda e, xr=xr, tb=tb: [e.dma_start(out=xr[:], in_=x_d[tb * 128:(tb + 1) * 128, :])],
               writes=[("xres", tb % 2)], dma="xres%d" % (tb % 2))
            for half in range(2):
                bk_ = nbank()

                for wq in range(4):
                    def fo(e, m=m, j=j, half=half, bk_=bk_, wq=wq):
                        r = None
                        for ei in range(4 * wq, 4 * wq + 4):
                            r = e.matmul(ps[:, bk_, :], m[:, ei, j * 128:(j + 1) * 128], wo_q[ei // 4][:, ei % 4, half * 512:(half + 1) * 512],
                                         start=(ei == 0), stop=(ei == 15))
                        return r
                    op("pe", fo, reads=[mres, ("wo", wq)], writes=[("ps", bk_)])
                op("dve", lambda e, xr=xr, orr=orr, half=half, bk_=bk_: e.tensor_tensor(
                    out=orr[:, half * 512:(half + 1) * 512], in0=ps[:, bk_, :], in1=xr[:, half * 512:(half + 1) * 512], op=ALU.add),
                   reads=[("ps", bk_), ("xres", tb % 2)], writes=[("ores", tb % 2)])
            st = op("act", lambda e, orr=orr, tb=tb: [e.dma_start(out=out_d[tb * 128:(tb + 1) * 128, :], in_=orr[:])],
                    reads=[("ores", tb % 2)], writes=[("outd", tb)], dma="ores%d" % (tb % 2))
            last_stores.append(st)
    return _finish(nc, sch, dbg_d, None, ps, None)


def _finish(nc, sch, dbg_d, dump, ps, _):
    sch.barrier()
    sch.op("sp", lambda e: None)
    sch.op("pe", lambda e: None)
    sch.op("act", lambda e: None)
    sch.op("dve", lambda e: None)
    sch.op("pool", lambda e: None)
    sch.finalize()
    with contextlib.ExitStack() as es:
        sems = {}
        for dom in sch.domops:
            nm = dom if isinstance(dom, str) else "d_" + dom[1]
            sems[dom] = es.enter_context(nc.semaphore(nm))
        blk = es.enter_context(nc.Block())

        @blk.sync
        def _(e):
            sch.emit("sp", e, sems)

        @blk.tensor
        def _(e):
            sch.emit("pe", e, sems)

        @blk.scalar
        def _(e):
            sch.emit("act", e, sems)

        @blk.vector
        def _(e):
            sch.emit("dve", e, sems)

        @blk.gpsimd
        def _(e):
            sch.emit("pool", e, sems)
    return nc


def _t5_bucket(n):
    n = np.asarray(n, dtype=np.int64)
    nf = np.maximum(n, 1).astype(np.float32)
    large = 16 + (np.log(nf / np.float32(16.0)) / np.float32(math.log(128 / 16)) * np.float32(16.0)).astype(np.int32)
    large = np.minimum(large, 31)
    return np.where(n < 16, n, large)


def _consts():
    e1 = np.zeros((33, 384), np.float32)
    for m in range(383):
        dist = m - 127
        if dist < 0:
            e1[32, m] = 1.0
        else:
            b = int(_t5_bucket(dist))
            e1[b, m] += 1.0
            e1[31, m] -= 1.0
    cst = np.zeros((128, 256), np.float32)
    cst[:, 0:128] = np.eye(128, dtype=np.float32)
    cst[:, 128:256] = np.eye(128, dtype=np.float32)[::-1]
    return e1, cst


_NC_CACHE = {}


def _prep_shared(inp):
    f = lambda a: np.ascontiguousarray(np.asarray(a, dtype=np.float32))
    cvec = np.zeros((128, NCV), np.float32)
    cvec[:, 0:8] = f(inp["norm_gain"])[0].reshape(8, 128).T
    cvec[:, 8:16] = f(inp["conv_b"])[0].reshape(8, 128).T
    cvec[:, 16:24] = f(inp["lru_lambda"])[0].reshape(8, 128).T
    cvec[:, 24:32] = f(inp["b_rg"])[0].T
    cvec[:, 32:40] = f(inp["b_ig"])[0].T
    cvec[:, 40:72] = f(inp["conv_w"])[0].reshape(4, 8, 128).transpose(2, 1, 0).reshape(128, 32)
    cvec[:, 72] = np.tile(f(inp["q_norm_gain"])[0], 2)
    cvec[:, 73] = np.tile(f(inp["k_norm_gain"])[0], 2)
    cvec[:, 74] = f(inp["subln_gain"])[0]
    lamv = np.zeros((128, 256), np.float32)
    for i, k in enumerate(("lambda_q1", "lambda_k1", "lambda_q2", "lambda_k2")):
        lamv[:, i * 64:(i + 1) * 64] = f(inp[k])[0][None, :]
    e1, cst = _consts()
    return {
        "w_in": f(inp["w_in"])[0],
        "w_out": f(inp["w_out"])[0],
        "cvec": cvec,
        "lamv": lamv,
        "relb": f(inp["rel_bias"]),
        "e1": e1,
        "cst": cst,
        "w_rg": np.ascontiguousarray(f(inp["w_rg"])[0].transpose(1, 0, 2)),
        "w_ig": np.ascontiguousarray(f(inp["w_ig"])[0].transpose(1, 0, 2)),
    }


def kernel(**inputs):
    shared = _prep_shared(inputs)
    x = np.asarray(inputs["x"], dtype=np.float32)
    if "nc" not in _NC_CACHE:
        _NC_CACHE["nc"] = build()
    nc = _NC_CACHE["nc"]
    in_maps = []
    for b in range(8):
        m = dict(shared)
        m["x"] = np.ascontiguousarray(x[b])
        in_maps.append(m)
    res = run_bass_kernel_spmd(nc, in_maps, core_ids=list(range(8)))
    return np.stack([np.asarray(r["out"], dtype=np.float32) for r in res.results], axis=0)
```

```python
import contextlib
import math

import numpy as np
import ml_dtypes
import concourse.bass as bass
import concourse.mybir as mybir
from concourse.bass_utils import run_bass_kernel_spmd

F32 = mybir.dt.float32
BF16 = mybir.dt.bfloat16
AF = mybir.ActivationFunctionType
ALU = mybir.AluOpType
AX = mybir.AxisListType

S = 4096
D = 1024
DIN = 6144
OFF_LX, OFF_LG, OFF_Q, OFF_K, OFF_V, OFF_AG = 0, 1024, 2048, 3072, 4096, 5120
EPS = 1e-6
LAM_INIT = 0.8 - 0.6 * math.exp(-0.3 * 0)
NCV = 80
MASKV = -30000.0
SAME_ENG_SYNC = True


class _Op:
    __slots__ = ("eng", "fn", "deps", "signal", "value", "dom", "ndma", "idx")


class Sched:
    ENGS = ("pe", "act", "dve", "pool", "sp")

    def __init__(self):
        self.q = {e: [] for e in self.ENGS}
        self.lastw = {}
        self.lastr = {}
        self.domops = {}
        self.bar = {}

    def op(self, eng, fn, reads=(), writes=(), dma=None, ndma=1):
        o = _Op()
        o.eng = eng
        o.fn = fn
        o.signal = False
        o.value = 0
        o.ndma = ndma
        o.dom = ("dma", dma) if dma is not None else eng
        deps = dict(self.bar)

        def add(d):
            if d is None:
                return
            k = d.dom
            if k not in deps or deps[k].idx < d.idx:
                deps[k] = d

        for r in reads:
            add(self.lastw.get(r))
        for r in writes:
            add(self.lastw.get(r))
            for d in self.lastr.get(r, {}).values():
                add(d)
        lst = self.domops.setdefault(o.dom, [])
        o.idx = len(lst)
        lst.append(o)
        for r in writes:
            self.lastw[r] = o
            self.lastr[r] = {}
        for r in reads:
            self.lastr.setdefault(r, {})[o.dom] = o
        o.deps = list(deps.values())
        self.q[eng].append(o)
        return o

    def barrier(self):
        self.bar = {dom: lst[-1] for dom, lst in self.domops.items() if lst}

    @staticmethod
    def _skip(o, d):
        return d.dom == o.dom and not isinstance(d.dom, tuple) and (o.eng == "pe" or not SAME_ENG_SYNC)

    def finalize(self):
        for ops in self.q.values():
            for o in ops:
                for d in o.deps:
                    if not self._skip(o, d):
                        d.signal = True
        for dom, lst in self.domops.items():
            c = 0
            for o in lst:
                if isinstance(dom, tuple):
                    c += 16 * o.ndma
                    o.value = c
                elif o.signal:
                    c += 1
                    o.value = c

    def emit(self, name, eng, sems):
        waited = {}
        for o in self.q[name]:
            for d in o.deps:
                if self._skip(o, d):
                    continue
                if waited.get(d.dom, 0) >= d.value:
                    continue
                eng.wait_ge(sems[d.dom], d.value)
                waited[d.dom] = d.value
            r = o.fn(eng)
            if isinstance(o.dom, tuple):
                for i in r:
                    i.then_inc(sems[o.dom], 16)
            elif o.signal:
                r.then_inc(sems[o.dom], 1)


def build(stop=None, dbg=False):
    nc = bass.Bass("TRN2", target_bir_lowering=False)

    def dram(name, shape, dtype, kind):
        return nc.dram_tensor(name, shape, dtype, kind=kind).ap()

    x_d = dram("x", [S, D], F32, "ExternalInput")
    win_d = dram("w_in", [D, DIN], F32, "ExternalInput")
    wout_d = dram("w_out", [2048, D], F32, "ExternalInput")
    cvec_d = dram("cvec", [128, NCV], F32, "ExternalInput")
    lamv_d = dram("lamv", [128, 256], F32, "ExternalInput")
    relb_d = dram("relb", [32, 8], F32, "ExternalInput")
    e1_d = dram("e1", [33, 384], F32, "ExternalInput")
    cst_d = dram("cst", [128, 256], F32, "ExternalInput")
    wrg_d = dram("w_rg", [128, 8, 128], F32, "ExternalInput")
    wig_d = dram("w_ig", [128, 8, 128], F32, "ExternalInput")
    out_d = dram("out", [S, D], F32, "ExternalOutput")
    mix_d = dram("mixscr", [16, 128, S], BF16, "ExternalOutput" if dbg else "Internal")
    vrow_d = dram("vrow", [8, 384], F32, "Internal")
    dbg_d = None

    SB0 = 16512
    SBTOP = 229344
    cur = [SB0]
    nid = [0]

    def sb(shape, dtype, at=None):
        nbytes = int(np.prod(shape[1:])) * (4 if dtype == F32 else 2)
        nbytes = (nbytes + 63) // 64 * 64
        if at is None:
            off = cur[0]
            cur[0] += nbytes
        else:
            off = at[0]
            at[0] += nbytes
        assert off + nbytes <= SBTOP, ("SBUF overflow", off + nbytes - SBTOP)
        nid[0] += 1
        return nc.alloc_sbuf_tensor_at("t%d" % nid[0], list(shape), dtype, offset=off)

    ps = nc.alloc_psum_tensor("ps", [128, 8, 512], F32)

    xT = sb([128, 8, S], BF16)
    cst = sb([128, 256], F32)
    ident = cst[:, 0:128]
    jmat = cst[:, 128:256]
    ones_f = sb([128, 128], F32)
    bones = sb([128, 128], BF16)
    cvec = sb([128, NCV], F32)
    lamv = sb([128, 256], F32)
    small = sb([128, 160], F32)
    bhi = sb([128, 8, 256], BF16)
    blo = sb([128, 8, 256], BF16)
    identb = sb([128, 128], BF16)
    PH0 = cur[0]
    PHTOP = (SBTOP // 64) * 64 - 8192 - 6144
    at_top = [PHTOP]
    wh0_top = sb([128, 3, 8, 128], BF16, at_top)
    wv = sb([128, 8, 512], BF16, at_top)
    wh = [wh0_top, None]

    def load_wh(h):
        srcs = [OFF_Q + h * 128, OFF_K + h * 128, OFF_AG + h * 128]
        op("pool", lambda e, h=h, srcs=srcs: [
            e.dma_start(out=wh[h % 2][:, i, :, :], in_=win_d[:, srcs[i]:srcs[i] + 128].rearrange("(k p) n -> p k n", p=128)) for i in range(3)],
           writes=[("wh", h % 2)], dma="wh%d" % (h % 2), ndma=3)

    def load_wv(g):
        op("pool", lambda e, g=g: [e.dma_start(out=wv[:], in_=win_d[:, OFF_V + g * 512:OFF_V + (g + 1) * 512].rearrange("(k p) n -> p k n", p=128))],
           writes=["wv"], dma="wv")

    CV_NG, CV_CB, CV_LL, CV_BR, CV_BI, CV_CW, CV_QG, CV_KG, CV_SG = 0, 8, 16, 24, 32, 40, 72, 73, 74
    SM_SS, SM_RSTD, SM_HBR, SM_HBI, SM_CC, SM_HC, SM_GQ8, SM_NLAM, SM_T0, SM_LS = 0, 32, 64, 72, 80, 88, 96, 97, 100, 120

    sch = Sched()
    op = sch.op

    def col(t, c):
        return t[:, c:c + 1]

    op("sp", lambda e: [e.dma_start(out=cst[:], in_=cst_d)], writes=["cst"], dma="cst")
    op("sp", lambda e: [e.dma_start(out=cvec[:], in_=cvec_d)], writes=["cvec"], dma="cvec")
    op("sp", lambda e: [e.dma_start(out=lamv[:], in_=lamv_d)], writes=["lamv"], dma="lamv")
    op("dve", lambda e: e.memset(ones_f[:], 1.0), writes=["ones_f"])
    op("dve", lambda e: e.memset(bones[:], 0.0), writes=["bones"])
    op("dve", lambda e: e.memset(bones[0:64, 0:64], 1.0), writes=["bones"])
    op("dve", lambda e: e.memset(bones[64:128, 64:128], 1.0), writes=["bones"])

    at = [PH0]
    rbaug = sb([33, 8], F32, at)
    e1s = sb([33, 384], F32, at)
    vsb = sb([8, 384], F32, at)
    hk8 = sb([128, 8, 256], F32, at)
    lprod = sb([128, 128], F32, at)
    biasT = sb([128, 8, 256], F32, at)
    btmp = sb([128, 8, 256], F32, at)
    PHA = at[0]

    op("dve", lambda e: e.memset(rbaug[32:33, :], MASKV), writes=["rbaug"])
    op("sp", lambda e: [e.dma_start(out=rbaug[0:32, :], in_=relb_d)], writes=["rbaug"], dma="rbaug")
    op("sp", lambda e: [e.dma_start(out=e1s[:], in_=e1_d)], writes=["e1s"], dma="e1s")
    op("pe", lambda e: e.matmul(ps[0:8, 0, 0:384], rbaug[:, :], e1s[:, :], start=True, stop=True),
       reads=["rbaug", "e1s"], writes=[("ps", 0)])
    op("dve", lambda e: e.tensor_copy(out=vsb[:], in_=ps[0:8, 0, 0:384]), reads=[("ps", 0)], writes=["vsb"])
    op("sp", lambda e: [e.dma_start(out=vrow_d, in_=vsb[:])], reads=["vsb"], writes=["vrow_d"], dma="vrow")
    op("sp", lambda e: [e.dma_start(out=hk8[:], in_=bass.AP(vrow_d.tensor, 0, [[1, 128], [384, 8], [1, 256]]))],
       reads=["vrow_d"], writes=["hk8"], dma="hk8")
    for h in range(8):
        b = h % 4
        op("pe", lambda e, h=h, b=b: e.matmul(ps[:, b, 0:256], jmat, hk8[:, h, :], start=True, stop=True),
           reads=["hk8", "cst"], writes=[("ps", b)])
        op("dve", lambda e, h=h, b=b: e.tensor_copy(out=biasT[:, h, :], in_=ps[:, b, 0:256]),
           reads=[("ps", b)], writes=["biasT"])

    op("dve", lambda e: e.tensor_copy(out=bhi[:], in_=biasT[:]), reads=["biasT"], writes=["bhi"])
    op("dve", lambda e: e.tensor_tensor(out=btmp[:], in0=biasT[:], in1=bhi[:], op=ALU.subtract), reads=["biasT", "bhi"], writes=["btmp"])
    op("dve", lambda e: e.tensor_copy(out=blo[:], in_=btmp[:]), reads=["btmp"], writes=["blo"])
    op("dve", lambda e: e.tensor_copy(out=identb[:], in_=ident), reads=["cst"], writes=["identb"])
    op("dve", lambda e: e.tensor_tensor(out=lprod[:, 0:64], in0=lamv[:, 0:64], in1=lamv[:, 64:128], op=ALU.mult),
       reads=["lamv"], writes=["lprod"])
    op("dve", lambda e: e.tensor_tensor(out=lprod[:, 64:128], in0=lamv[:, 128:192], in1=lamv[:, 192:256], op=ALU.mult),
       reads=["lamv"], writes=["lprod"])
    op("dve", lambda e: e.tensor_reduce(out=small[:, SM_LS:SM_LS + 2], in_=lprod[:].rearrange("p (a b) -> p a b", a=2),
                                        axis=AX.X, op=ALU.add), reads=["lprod"], writes=["sm_ls"])
    op("act", lambda e: e.activation(out=small[:, SM_LS + 2:SM_LS + 4], in_=small[:, SM_LS:SM_LS + 2], func=AF.Exp),
       reads=["sm_ls"], writes=["sm_le"])
    op("dve", lambda e: e.tensor_tensor(out=col(small, SM_LS + 4), in0=col(small, SM_LS + 2), in1=col(small, SM_LS + 3),
                                        op=ALU.subtract), reads=["sm_le"], writes=["sm_l"])
    op("dve", lambda e: e.tensor_scalar(out=col(small, SM_NLAM), in0=col(small, SM_LS + 4), scalar1=LAM_INIT, scalar2=-1.0,
                                        op0=ALU.add, op1=ALU.mult), reads=["sm_l"], writes=["nlam"])
    op("act", lambda e: e.activation(out=small[:, SM_T0:SM_T0 + 8], in_=cvec[:, CV_LL:CV_LL + 8], func=AF.Exp, scale=-1.0),
       reads=["cvec"], writes=["sm_t0"])
    op("act", lambda e: e.activation(out=small[:, SM_T0 + 8:SM_T0 + 16], in_=small[:, SM_T0:SM_T0 + 8], func=AF.Ln, bias=1.0),
       reads=["sm_t0"], writes=["sm_t1"])
    op("dve", lambda e: e.tensor_scalar(out=small[:, SM_CC:SM_CC + 8], in0=small[:, SM_T0 + 8:SM_T0 + 16], scalar1=-8.0,
                                        scalar2=None, op0=ALU.mult), reads=["sm_t1"], writes=["lruc"])
    op("dve", lambda e: e.tensor_scalar(out=small[:, SM_HC:SM_HC + 8], in0=small[:, SM_T0 + 8:SM_T0 + 16], scalar1=-4.0,
                                        scalar2=None, op0=ALU.mult), reads=["sm_t1"], writes=["lruc"])
    op("dve", lambda e: e.tensor_scalar(out=small[:, SM_HBR:SM_HBR + 16], in0=cvec[:, CV_BR:CV_BR + 16], scalar1=0.5,
                                        scalar2=None, op0=ALU.mult), reads=["cvec"], writes=["lruc"])
    op("dve", lambda e: e.tensor_scalar(out=col(small, SM_GQ8), in0=col(cvec, CV_QG), scalar1=0.125, scalar2=None,
                                        op0=ALU.mult), reads=["cvec"], writes=["gq8"])

    at = [PHA]
    xa = [sb([128, 4, D], F32, at) for _ in range(4)]
    sqj = sb([128, D], BF16, at)
    dg = [sb([128, 128], F32, at) for _ in range(2)]
    lnt0 = sb([128, 32], F32, at)

    for tg in range(8):
        buf = xa[tg % 4]
        bk = [("xa", tg % 4, j) for j in range(4)]
        op("sp", lambda e, tg=tg, buf=buf: [e.dma_start(
            out=buf[:], in_=x_d[tg * 512:(tg + 1) * 512, :].rearrange("(j p) d -> p j d", p=128))],
           writes=bk, dma="xa%d" % (tg % 4))
        for j in range(4):
            tb = tg * 4 + j
            op("act", lambda e, buf=buf, j=j, tb=tb: e.activation(out=sqj[:], in_=buf[:, j, :], func=AF.Square,
                                                                  accum_out=col(small, SM_SS + tb)),
               reads=[bk[j]], writes=["sqj", ("ss", tg)])
        op("act", lambda e, tg=tg: e.activation(out=lnt0[:, tg * 4:tg * 4 + 4], in_=small[:, SM_SS + tg * 4:SM_SS + tg * 4 + 4],
                                               func=AF.Ln, scale=1.0 / D, bias=EPS), reads=[("ss", tg)], writes=[("lnt0", tg)])
        op("act", lambda e, tg=tg: e.activation(out=small[:, SM_RSTD + tg * 4:SM_RSTD + tg * 4 + 4], in_=lnt0[:, tg * 4:tg * 4 + 4],
                                               func=AF.Exp, scale=-0.5), reads=[("lnt0", tg)], writes=[("rstd", tg)])
        for j in range(4):
            tb = tg * 4 + j
            if j < 3:
                op("act", lambda e, buf=buf, j=j, tb=tb: e.activation(out=buf[:, j, :], in_=buf[:, j, :], func=AF.Copy,
                                                                      scale=col(small, SM_RSTD + tb)),
                   reads=[bk[j], ("rstd", tg)], writes=[bk[j]])
            else:
                op("dve", lambda e, buf=buf, j=j, tb=tb: e.tensor_scalar(out=buf[:, j, :], in0=buf[:, j, :], scalar1=col(small, SM_RSTD + tb),
                                                                         scalar2=1.0, op0=ALU.mult, op1=ALU.mult),
                   reads=[bk[j], ("rstd", tg)], writes=[bk[j]])
        for c in range(8):
            def tr(e, buf=buf, c=c):
                r = None
                for j in range(4):
                    r = e.transpose(ps[:, c, j * 128:(j + 1) * 128], buf[:, j, c * 128:(c + 1) * 128], ident)
                return r
            op("pe", tr, reads=bk + ["cst"], writes=[("ps", c)])
            op("dve", lambda e, c=c, tg=tg: e.tensor_scalar(out=xT[:, c, tg * 512:(tg + 1) * 512], in0=ps[:, c, :],
                                                            scalar1=col(cvec, CV_NG + c), scalar2=None, op0=ALU.mult),
               reads=[("ps", c), "cvec"], writes=[("xT", c, tg)])
    sch.barrier()

    if stop == "A":
        return _finish(nc, sch, dbg_d, None, ps, None)

    def proj_fm(wt, widx, tc, bank, wres):
        def f(e):
            r = None
            for k in range(8):
                r = e.matmul(ps[:, bank, :], wt[:, widx, k, :], xT[:, k, tc * 512:(tc + 1) * 512], start=(k == 0), stop=(k == 7))
            return r
        op("pe", f, reads=[wres] + [("xT", k, tc) for k in range(8)], writes=[("ps", bank)])

    bankc = [0]

    def nbank(n=8, lo=0):
        b = lo + bankc[0] % n
        bankc[0] += 1
        return b

    TL = 1024
    NSET = 2
    NQ = TL // 512
    NS = S // TL
    at = [PH0]
    wl = [sb([128, 2, 8, 128], BF16, at) for _ in range(2)]
    wrg = sb([128, 8, 128], BF16, at)
    wig = sb([128, 8, 128], BF16, at)
    dgw = [sb([128, 4, 128], BF16, at) for _ in range(2)]
    hcar = sb([128, 8], F32, at)

    def lset():
        return dict(xlb=sb([128, TL + 16], BF16, at), xlb2=sb([128, TL + 16], BF16, at), gl=sb([128, TL], F32, at), xc=sb([128, TL], F32, at),
                    xcb=sb([128, TL], BF16, at), rr=sb([128, TL], F32, at), ti=sb([128, TL], F32, at),
                    aa=sb([128, TL], F32, at), a2=sb([128, TL], F32, at), uu=sb([128, TL], F32, at),
                    hh=sb([128, TL], F32, at), tg=sb([128, TL], F32, at), ylb=sb([128, TL], BF16, at))
    LS = [lset() for _ in range(NSET)]
    assert at[0] <= PHTOP, (at[0], PHTOP)

    op("pool", lambda e: [e.dma_start(out=wrg[:], in_=wrg_d)], writes=["wrg"], dma="wrg")
    op("pool", lambda e: [e.dma_start(out=wig[:], in_=wig_d)], writes=["wig"], dma="wig")

    def load_wl(c):
        w = wl[c % 2]
        op("pool", lambda e, c=c, w=w: [
            e.dma_start(out=w[:, 0, :, :], in_=win_d[:, OFF_LX + c * 128:OFF_LX + (c + 1) * 128].rearrange("(k p) n -> p k n", p=128)),
            e.dma_start(out=w[:, 1, :, :], in_=win_d[:, OFF_LG + c * 128:OFF_LG + (c + 1) * 128].rearrange("(k p) n -> p k n", p=128)),
        ], writes=[("wl", c % 2)], dma="wl%d" % (c % 2), ndma=2)

    def lru_front(u):
        c, s_ = divmod(u, NS)
        p = u % NSET
        po = (u - 1) % NSET
        B = LS[p]
        w = wl[c % 2]
        wres = ("wl", c % 2)
        R = lambda n: ("L", n, p)
        if s_ == 0:
            if c + 1 < 8:
                load_wl(c + 1)
            for j in range(4):
                op("dve", lambda e, j=j, c=c: e.tensor_scalar(out=dgw[c % 2][:, j, :], in0=ident, scalar1=col(cvec, CV_CW + c * 4 + j),
                                                              scalar2=None, op0=ALU.mult), reads=["cst", "cvec"], writes=[("dgw", c % 2)])
            op("dve", lambda e: e.memset(B["xlb"][:, 4:8], 0.0), writes=[R("xlb")])
        else:
            Bo = LS[po]
            op("dve", lambda e: e.tensor_copy(out=B["xlb"][:, 4:8], in_=Bo["xlb"][:, TL + 4:TL + 8]), reads=[("L", "xlb", po)], writes=[R("xlb")])
        for q in range(NQ):
            tc = s_ * NQ + q
            sl = slice(q * 512, (q + 1) * 512)
            b1 = nbank()
            proj_fm(w, 0, tc, b1, wres)
            op("dve", lambda e, q=q, b1=b1: e.tensor_copy(out=B["xlb"][:, 8 + q * 512:8 + (q + 1) * 512], in_=ps[:, b1, :]),
               reads=[("ps", b1)], writes=[R("xlb")])
            b2 = nbank()
            proj_fm(w, 1, tc, b2, wres)
            op("act", lambda e, sl=sl, b2=b2: e.activation(out=B["gl"][:, sl], in_=ps[:, b2, :], func=AF.Copy),
               reads=[("ps", b2)], writes=[R("gl")])
        op("dve", lambda e: e.tensor_copy(out=B["xlb2"][:, 5:TL + 9], in_=B["xlb"][:, 4:TL + 8]), reads=[R("xlb")], writes=[R("xlb2")])
        taps = [("xlb2", 6), ("xlb", 6), ("xlb2", 8), ("xlb", 8)]
        for q in range(NQ):
            sl = slice(q * 512, (q + 1) * 512)
            b3 = nbank()

            def fcv(e, q=q, b3=b3, c=c):
                r = None
                for j in range(4):
                    nm, o_ = taps[j]
                    r = e.matmul(ps[:, b3, :], dgw[c % 2][:, j, :], B[nm][:, o_ + q * 512:o_ + (q + 1) * 512],
                                 start=(j == 0), stop=(j == 3))
                return r
            op("pe", fcv, reads=[R("xlb"), R("xlb2"), ("dgw", c % 2)], writes=[("ps", b3)])
            op("dve", lambda e, sl=sl, b3=b3, c=c: e.tensor_scalar(out=B["xcb"][:, sl], in0=ps[:, b3, :], scalar1=col(cvec, CV_CB + c),
                                                                   scalar2=None, op0=ALU.add), reads=[("ps", b3), "cvec"], writes=[R("xcb")])
        for q in range(NQ):
            sl = slice(q * 512, (q + 1) * 512)
            b1 = nbank()
            op("pe", lambda e, c=c, sl=sl, b1=b1: e.matmul(ps[:, b1, :], wrg[:, c, :], B["xcb"][:, sl], start=True, stop=True),
               reads=["wrg", R("xcb")], writes=[("ps", b1)])
            op("act", lambda e, c=c, sl=sl, b1=b1: e.activation(out=B["rr"][:, sl], in_=ps[:, b1, :], func=AF.Tanh, scale=0.5,
                                                                bias=col(small, SM_HBR + c)), reads=[("ps", b1), "lruc"], writes=[R("rr")])
            b2 = nbank()
            op("pe", lambda e, c=c, sl=sl, b2=b2: e.matmul(ps[:, b2, :], wig[:, c, :], B["xcb"][:, sl], start=True, stop=True),
               reads=["wig", R("xcb")], writes=[("ps", b2)])
            op("act", lambda e, c=c, sl=sl, b2=b2: e.activation(out=B["ti"][:, sl], in_=ps[:, b2, :], func=AF.Tanh, scale=0.5,
                                                                bias=col(small, SM_HBI + c)), reads=[("ps", b2), "lruc"], writes=[R("ti")])
        op("act", lambda e: e.activation(out=B["tg"][:], in_=B["gl"][:], func=AF.Silu), reads=[R("gl")], writes=[R("tg")])
        op("act", lambda e, c=c: e.activation(out=B["aa"][:], in_=B["rr"][:], func=AF.Exp, scale=col(small, SM_HC + c), bias=col(small, SM_HC + c)),
           reads=[R("rr"), "lruc"], writes=[R("aa")])
        op("act", lambda e, c=c: e.activation(out=B["a2"][:], in_=B["rr"][:], func=AF.Exp, scale=col(small, SM_CC + c), bias=col(small, SM_CC + c)),
           reads=[R("rr"), "lruc"], writes=[R("a2")])
        op("act", lambda e: e.activation(out=B["a2"][:], in_=B["a2"][:], func=AF.Ln, scale=-1.0, bias=1.0), reads=[R("a2")], writes=[R("a2")])
        op("act", lambda e: e.activation(out=B["a2"][:], in_=B["a2"][:], func=AF.Exp, scale=0.5, bias=math.log(0.5)), reads=[R("a2")], writes=[R("a2")])

    def lru_back(u):
        c, s_ = divmod(u, NS)
        p = u % NSET
        B = LS[p]
        R = lambda n: ("L", n, p)
        t0 = s_ * TL
        op("dve", lambda e: e.scalar_tensor_tensor(out=B["uu"][:], in0=B["ti"][:], scalar=1.0, in1=B["xcb"][:], op0=ALU.add, op1=ALU.mult),
           reads=[R("ti"), R("xcb")], writes=[R("uu")])
        op("dve", lambda e: e.tensor_tensor(out=B["uu"][:], in0=B["uu"][:], in1=B["a2"][:], op=ALU.mult), reads=[R("uu"), R("a2")], writes=[R("uu")])
        if s_ == 0:
            op("dve", lambda e: e.tensor_tensor_scan(out=B["hh"][:], data0=B["aa"][:], data1=B["uu"][:], initial=0.0, op0=ALU.mult, op1=ALU.add),
               reads=[R("aa"), R("uu")], writes=[R("hh")])
        else:
            op("dve", lambda e: e.tensor_tensor_scan(out=B["hh"][:], data0=B["aa"][:], data1=B["uu"][:], initial=hcar[:, 0:1], op0=ALU.mult,
                                                     op1=ALU.add), reads=[R("aa"), R("uu"), "hcar"], writes=[R("hh")])
        if s_ + 1 < NS:
            op("dve", lambda e: e.tensor_copy(out=hcar[:, 0:1], in_=B["hh"][:, TL - 1:TL]), reads=[R("hh")], writes=["hcar"])
        op("dve", lambda e: e.tensor_tensor(out=B["ylb"][:], in0=B["hh"][:], in1=B["tg"][:], op=ALU.mult),
           reads=[R("hh"), R("tg")], writes=[R("ylb")])
        op("sp", lambda e, c=c, t0=t0: [e.dma_start(out=mix_d[c, :, t0:t0 + TL], in_=B["ylb"][:])],
           reads=[R("ylb")], writes=[("mixd", c)], dma="ylb%d" % p)

    load_wl(0)
    NU = 8 * NS
    for u in range(NU + NSET - 1):
        if u < NU:
            lru_front(u)
        if u >= NSET - 1:
            lru_back(u - (NSET - 1))
    load_wh(0)
    load_wv(0)
    sch.barrier()

    if stop == "B":
        return _finish(nc, sch, dbg_d, None, ps, None)

    at = [PH0]
    Vc = sb([128, 32, 4, 130], BF16, at)
    qT = sb([128, S], BF16, at)
    kT = sb([128, S], BF16, at)
    sgT = sb([128, S], BF16, at)
    wh[1] = sb([128, 3, 8, 128], BF16, at)
    NZ = 4
    NB_IP = 7
    zsq = [sb([128, 512], BF16, at) for _ in range(NZ)]
    rq = [sb([128, 512], F32, at) for _ in range(NZ)]
    NPT = 4
    pt = [sb([128, 2, 512], BF16, at) for _ in range(NPT)]
    accs2 = [sb([128, 8 * 129], F32, at) for _ in range(2)]
    rden2 = [sb([128, 8], F32, at) for _ in range(2)]
    ot2 = [sb([128, 4, 128], F32, at) for _ in range(2)]
    junk = sb([128, 128], F32, at)
    ssq2_ = [sb([128, 8], F32, at) for _ in range(2)]
    yn2 = [sb([128, 4, 128], BF16, at) for _ in range(2)]
    ps7b = ps[:, 7, :].bitcast(BF16)
    yat = [sb([128, 512], BF16, at) for _ in range(2)]
    assert at[0] <= PHTOP, (at[0], PHTOP)

    def _addr(t):
        return t[:].tensor.manual_sbuf_range[0]
    assert _addr(zsq[0]) == _addr(wh[1]) + 6144 and _addr(rq[3]) == _addr(rq[0]) + 3 * 2048 and _addr(pt[3]) == _addr(pt[0]) + 3 * 2048
    wo_q = [nc.alloc_sbuf_tensor_at("woq%d" % i, [128, 4, D], BF16, offset=a_)
            for i, a_ in enumerate((_addr(wh[1]), _addr(rq[0]), _addr(wv), _addr(pt[0])))]
    wo_res = [[("wh", 1), ("zsq", 0), ("zsq", 1)], [("rq", i) for i in range(NZ)], ["wv"], [("pt", i) for i in range(NPT)]]

    def load_wo(wq):
        op("pool", lambda e: [e.dma_start(out=wo_q[wq][:], in_=wout_d[512 * wq:512 * (wq + 1), :].rearrange("(k p) n -> p k n", p=128))],
           writes=[("wo", wq)] + wo_res[wq], dma="wo%d" % wq)

    op("dve", lambda e: e.memset(Vc[:, :, :, 128:130], 1.0), writes=["Vc"])

    zi = [0]
    carry = []
    yidx = 0
    for h in range(8):
        if h + 1 < 8:
            load_wh(h + 1)
        w = wh[h % 2]
        wres = ("wh", h % 2)
        vgroups = []
        if h == 1:
            load_wv(1)
        if h % 4 == 0:

            def vgroup(tb):
                bk_ = nbank(NB_IP)

                def fv(e):
                    r = None
                    for k in range(8):
                        r = e.matmul(ps[:, bk_, :], xT[:, k, tb * 128:(tb + 1) * 128], wv[:, k, :], start=(k == 0), stop=(k == 7))
                    return r
                op("pe", fv, reads=["wv"] + [("xT", k, tb // 4) for k in range(8)], writes=[("ps", bk_)])
                op("dve", lambda e: e.tensor_copy(out=Vc[:, tb, :, 0:128], in_=ps[:, bk_, :].rearrange("p (a b) -> p a b", a=4)),
                   reads=[("ps", bk_)], writes=["Vc"])
            for tb_ in range(32):
                vgroup(tb_)
        units = [(which, tc) for which in (0, 1) for tc in range(8)]
        st = {}

        def ip1(which, tc):
            i2 = zi[0] % NZ
            zi[0] += 1
            b1 = nbank(NB_IP)
            st[(which, tc)] = (i2, b1)
            proj_fm(w, which, tc, b1, wres)
            op("act", lambda e: e.activation(out=zsq[i2][:], in_=ps[:, b1, :], func=AF.Square),
               reads=[("ps", b1)], writes=[("zsq", i2)])

        def ip2(which, tc):
            i2, b1 = st[(which, tc)]
            dst = qT if which == 0 else kT
            b2 = nbank(NB_IP)
            op("pe", lambda e: e.matmul(ps[:, b2, :], bones[:], zsq[i2][:], start=True, stop=True),
               reads=["bones", ("zsq", i2)], writes=[("ps", b2)])
            op("act", lambda e: e.activation(out=rq[i2][:], in_=ps[:, b2, :], func=AF.Ln, scale=1.0 / 64, bias=EPS),
               reads=[("ps", b2)], writes=[("rq", i2)])
            op("act", lambda e: e.activation(out=rq[i2][:], in_=rq[i2][:], func=AF.Exp, scale=-0.5),
               reads=[("rq", i2)], writes=[("rq", i2)])
            gsc = col(small, SM_GQ8) if which == 0 else col(cvec, CV_KG)
            op("dve", lambda e: e.scalar_tensor_tensor(
                out=dst[:, tc * 512:(tc + 1) * 512], in0=ps[:, b1, :], scalar=gsc, in1=rq[i2][:], op0=ALU.mult, op1=ALU.mult),
               reads=[("ps", b1), ("rq", i2), "gq8", "cvec"], writes=[("qk", which, tc)])

        for ui in range(len(units) + 1):
            if ui < len(units):
                ip1(*units[ui])
            if ui >= 1:
                ip2(*units[ui - 1])
            if ui >= 3 and carry:
                carry.pop(0)()
        while carry:
            carry.pop(0)()
        while vgroups:
            vgroup(vgroups.pop(0))
        for tc in range(8):
            i2 = zi[0] % NZ
            zi[0] += 1
            b1 = nbank(NB_IP)
            proj_fm(w, 2, tc, b1, wres)
            op("act", lambda e, tc=tc, b1=b1: e.activation(out=sgT[:, tc * 512:(tc + 1) * 512], in_=ps[:, b1, :], func=AF.Silu),
               reads=[("ps", b1)], writes=[("sg", tc)])
        if h == 7:
            for wq_ in range(3):
                load_wo(wq_)
        hv = h % 4
        QORDER = (0, 1, 2, 3, 4, 5, 6, 7)
        steps = [(qc, kb) for qc in QORDER for kb in range(4 * qc + 4)]
        nst = len(steps)
        deferred = []

        def s_qk(i, h=h):
            qc, kb = steps[i]
            j0 = max(0, kb - 4 * qc)
            c0 = j0 * 128
            sb_ = (i % 2) * 2
            sres = [("ps", sb_), ("ps", sb_ + 1)]

            near = kb >= 4 * qc - 1
            if near:
                if kb == 4 * qc - 1:
                    cc0, wdt, bc0 = 0, 128, 128
                else:
                    cc0, wdt, bc0 = c0, min(256, 512 - c0), 0

            def fqk(e):
                e.matmul(ps[:, sb_, c0:512], kT[0:64, kb * 128:(kb + 1) * 128], qT[0:64, qc * 512 + c0:(qc + 1) * 512],
                         start=True, stop=not near, skip_group_check=True)
                r = e.matmul(ps[:, sb_ + 1, c0:512], kT[64:128, kb * 128:(kb + 1) * 128],
                             qT[64:128, qc * 512 + c0:(qc + 1) * 512], start=True, stop=not near, skip_group_check=True)
                if near:
                    for comp in range(2):
                        e.matmul(ps[:, sb_ + comp, cc0:cc0 + wdt], identb[:], bhi[:, h, bc0:bc0 + wdt], start=False, stop=False,
                                 skip_group_check=True)
                        r = e.matmul(ps[:, sb_ + comp, cc0:cc0 + wdt], identb[:], blo[:, h, bc0:bc0 + wdt], start=False, stop=True,
                                     skip_group_check=True)
                return r
            op("pe", fqk, reads=[("qk", 0, qc), ("qk", 1, kb // 4), "bhi", "blo", "identb"], writes=sres)
            p_ = pt[i % NPT]
            op("act", lambda e: e.activation(out=p_[:, :, c0:512], in_=ps[:, sb_:sb_ + 2, c0:512], func=AF.Exp),
               reads=sres, writes=[("pt", i % NPT)])

        def s_pv(i, hv=hv):
            qc, kb = steps[i]
            j0 = max(0, kb - 4 * qc)
            p_ = pt[i % NPT]

            def fpv(e):
                r = None
                for j in range(j0, 4):
                    for comp in range(2):
                        a = j * 2 + comp
                        bank = 4 + a // 3
                        o0 = (a % 3) * 129
                        r = e.matmul(ps[:, bank, o0:o0 + 129], p_[:, comp, j * 128:(j + 1) * 128], Vc[:, kb, hv, 0:129],
                                     start=(kb == 0 and a % 3 == 0), stop=(kb == 4 * qc + j), skip_group_check=True)
                return r
            touched = sorted({4 + (j * 2 + comp) // 3 for j in range(j0, 4) for comp in range(2)})
            op("pe", fpv, reads=[("pt", i % NPT), "Vc"], writes=[("ps", b_) for b_ in touched])

        def acc_copy(bnk, ts_):
            accs = accs2[ts_]
            n = 3 if bnk < 2 else 2
            op("dve", lambda e: e.tensor_copy(out=accs[:, bnk * 387:bnk * 387 + n * 129], in_=ps[:, 4 + bnk, 0:n * 129]),
               reads=[("ps", 4 + bnk)], writes=[("accs", ts_)])

        def tail1(qc, ts_):
            accs, rden, ot, ssq = accs2[ts_], rden2[ts_], ot2[ts_], ssq2_[ts_]
            av = accs[:].rearrange("p (a b) -> p a b", a=8)
            rv = rden[:].rearrange("p (j c) -> p j c", c=2)
            op("dve", lambda e: e.reciprocal(out=rden[:], in_=av[:, :, 128]), reads=[("accs", ts_)], writes=[("rden", ts_)])
            op("dve", lambda e: e.tensor_scalar(out=rv[:, :, 1], in0=rv[:, :, 1], scalar1=col(small, SM_NLAM), scalar2=None,
                                                op0=ALU.mult), reads=[("rden", ts_), "nlam"], writes=[("rden", ts_)])
            for j in range(4):
                op("dve", lambda e, j=j: e.tensor_scalar(out=ot[:, j, :], in0=av[:, 2 * j, 0:128], scalar1=rden[:, 2 * j:2 * j + 1],
                                                         scalar2=None, op0=ALU.mult), reads=[("accs", ts_), ("rden", ts_)], writes=[("ot", ts_, j)])
                op("dve", lambda e, j=j: e.scalar_tensor_tensor(out=ot[:, j, :], in0=av[:, 2 * j + 1, 0:128],
                                                                scalar=rden[:, 2 * j + 1:2 * j + 2], in1=ot[:, j, :],
                                                                op0=ALU.mult, op1=ALU.add),
                   reads=[("accs", ts_), ("rden", ts_), ("ot", ts_, j)], writes=[("ot", ts_, j)])
                op("dve", lambda e, j=j: e.scalar_tensor_tensor(out=junk[:], in0=ot[:, j, :], scalar=1.0, in1=ot[:, j, :],
                                                                op0=ALU.mult, op1=ALU.mult, accum_out=ssq[:, j:j + 1]),
                   reads=[("ot", ts_, j)], writes=["junk", ("ssq", ts_, j)])

        def tail2(qc, ts_, h=h):
            ot, ssq, yn = ot2[ts_], ssq2_[ts_], yn2[ts_]
            op("act", lambda e: e.activation(out=ssq[:, 4:8], in_=ssq[:, 0:4], func=AF.Ln, scale=1.0 / 128, bias=EPS),
               reads=[("ssq", ts_, j) for j in range(4)], writes=[("ssqr", ts_)])
            op("act", lambda e: e.activation(out=ssq[:, 4:8], in_=ssq[:, 4:8], func=AF.Exp, scale=-0.5),
               reads=[("ssqr", ts_)], writes=[("ssqr", ts_)])
            for j in range(4):
                op("dve", lambda e, j=j: e.tensor_scalar(out=yn[:, j, :], in0=ot[:, j, :], scalar1=ssq[:, 4 + j:5 + j],
                                                         scalar2=(1.0 - LAM_INIT), op0=ALU.mult, op1=ALU.mult),
                   reads=[("ot", ts_, j), ("ssqr", ts_)], writes=[("yn", ts_)])

        def tail3(qc, ts_, h=h):
            nonlocal yidx
            yn = yn2[ts_]

            def ftr(e):
                r = None
                for j in range(4):
                    r = e.transpose(ps7b[:, j * 128:(j + 1) * 128], yn[:, j, :], identb[:])
                return r
            op("pe", ftr, reads=[("yn", ts_), "identb"], writes=[("ps", 7)])
            ya = yat[yidx % 2]
            yres = ("yat", yidx % 2)
            op("dve", lambda e: e.scalar_tensor_tensor(out=ya[:], in0=ps7b[:, 0:512], scalar=col(cvec, CV_SG),
                                                       in1=sgT[:, qc * 512:(qc + 1) * 512], op0=ALU.mult, op1=ALU.mult),
               reads=[("ps", 7), "cvec", ("sg", qc)], writes=[yres])
            op("sp", lambda e: [e.dma_start(out=mix_d[8 + h, :, qc * 512:(qc + 1) * 512], in_=ya[:])],
               reads=[yres], writes=[("mixd", 8 + h)], dma="yat%d" % (yidx % 2))
            yidx += 1

        TD = 5
        def la_of(s_):
            qc_, kb_ = steps[s_]
            return 3 if (kb_ == 0 and s_ > 0) else 2
        pvn = 0
        i = 0
        while pvn < nst:
            if i < nst:
                s_qk(i)
            for dd in [d_ for d_ in deferred if d_[0] <= i]:
                dd[1]()
                deferred.remove(dd)
            while pvn < nst and (pvn <= i - la_of(pvn) or i >= nst + 3):
                s_pv(pvn)
                qc_, kb_ = steps[pvn]
                tsi = QORDER.index(qc_) % 2
                if kb_ >= 4 * qc_ + 1:
                    acc_copy(kb_ - 4 * qc_ - 1, tsi)
                if kb_ == 4 * qc_ + 3:
                    tail1(qc_, tsi)
                    deferred.append((i + TD, lambda qc_=qc_, tsi=tsi, f_=tail2: f_(qc_, tsi)))
                    deferred.append((i + TD + 4, lambda qc_=qc_, tsi=tsi, f_=tail3: f_(qc_, tsi)))
                pvn += 1
            i += 1
        carry[:] = [dd[1] for dd in deferred]
        if h == 7:
            load_wo(3)
            for f_ in carry:
                f_()
    sch.barrier()

    if stop == "C":
        return _finish(nc, sch, dbg_d, None, ps, None)

    at = [PH0]
    mixc = [sb([128, 16, 512], BF16, at) for _ in range(2)]
    xres = [sb([128, D], F32, at) for _ in range(2)]
    ores = [sb([128, D], F32, at) for _ in range(2)]
    assert at[0] <= _addr(wh[1]), (at[0], _addr(wh[1]))
    last_stores = []
    for tc in range(8):
        m = mixc[tc % 2]
        mres = ("mixc", tc % 2)
        op("sp", lambda e, m=m, tc=tc: [e.dma_start(out=m[:], in_=mix_d[:, :, tc * 512:(tc + 1) * 512].rearrange("e p t -> p e t"))],
           reads=[("mixd", i) for i in range(16)], writes=[mres], dma="mixc%d" % (tc % 2))
        for j in range(4):
            tb = tc * 4 + j
            xr = xres[tb % 2]
            orr = ores[tb % 2]
            op("sp", lambda e, xr=xr, tb=tb: [e.dma_start(out=xr[:], in_=x_d[tb * 128:(tb + 1) * 128, :])],
               writes=[("xres", tb % 2)], dma="xres%d" % (tb % 2))
            for half in range(2):
                bk_ = nbank()

                for wq in range(4):
                    def fo(e, m=m, j=j, half=half, bk_=bk_, wq=wq):
                        r = None
                        for ei in range(4 * wq, 4 * wq + 4):
                            r = e.matmul(ps[:, bk_, :], m[:, ei, j * 128:(j + 1) * 128], wo_q[ei // 4][:, ei % 4, half * 512:(half + 1) * 512],
                                         start=(ei == 0), stop=(ei == 15))
                        return r
                    op("pe", fo, reads=[mres, ("wo", wq)], writes=[("ps", bk_)])
                op("dve", lambda e, xr=xr, orr=orr, half=half, bk_=bk_: e.tensor_tensor(
                    out=orr[:, half * 512:(half + 1) * 512], in0=ps[:, bk_, :], in1=xr[:, half * 512:(half + 1) * 512], op=ALU.add),
                   reads=[("ps", bk_), ("xres", tb % 2)], writes=[("ores", tb % 2)])
            st = op("act", lambda e, orr=orr, tb=tb: [e.dma_start(out=out_d[tb * 128:(tb + 1) * 128, :], in_=orr[:])],
                    reads=[("ores", tb % 2)], writes=[("outd", tb)], dma="ores%d" % (tb % 2))
            last_stores.append(st)
    return _finish(nc, sch, dbg_d, None, ps, None)


def _finish(nc, sch, dbg_d, dump, ps, _):
    sch.barrier()
    sch.op("sp", lambda e: None)
    sch.op("pe", lambda e: None)
    sch.op("act", lambda e: None)
    sch.op("dve", lambda e: None)
    sch.op("pool", lambda e: None)
    sch.finalize()
    with contextlib.ExitStack() as es:
        sems = {}
        for dom in sch.domops:
            nm = dom if isinstance(dom, str) else "d_" + dom[1]
            sems[dom] = es.enter_context(nc.semaphore(nm))
        blk = es.enter_context(nc.Block())

        @blk.sync
        def _(e):
            sch.emit("sp", e, sems)

        @blk.tensor
        def _(e):
            sch.emit("pe", e, sems)

        @blk.scalar
        def _(e):
            sch.emit("act", e, sems)

        @blk.vector
        def _(e):
            sch.emit("dve", e, sems)

        @blk.gpsimd
        def _(e):
            sch.emit("pool", e, sems)
    return nc


def _t5_bucket(n):
    n = np.asarray(n, dtype=np.int64)
    nf = np.maximum(n, 1).astype(np.float32)
    large = 16 + (np.log(nf / np.float32(16.0)) / np.float32(math.log(128 / 16)) * np.float32(16.0)).astype(np.int32)
    large = np.minimum(large, 31)
    return np.where(n < 16, n, large)


def _consts():
    e1 = np.zeros((33, 384), np.float32)
    for m in range(383):
        dist = m - 127
        if dist < 0:
            e1[32, m] = 1.0
        else:
            b = int(_t5_bucket(dist))
            e1[b, m] += 1.0
            e1[31, m] -= 1.0
    cst = np.zeros((128, 256), np.float32)
    cst[:, 0:128] = np.eye(128, dtype=np.float32)
    cst[:, 128:256] = np.eye(128, dtype=np.float32)[::-1]
    return e1, cst


_NC_CACHE = {}


def _prep_shared(inp):
    f = lambda a: np.ascontiguousarray(np.asarray(a, dtype=np.float32))
    cvec = np.zeros((128, NCV), np.float32)
    cvec[:, 0:8] = f(inp["norm_gain"])[0].reshape(8, 128).T
    cvec[:, 8:16] = f(inp["conv_b"])[0].reshape(8, 128).T
    cvec[:, 16:24] = f(inp["lru_lambda"])[0].reshape(8, 128).T
    cvec[:, 24:32] = f(inp["b_rg"])[0].T
    cvec[:, 32:40] = f(inp["b_ig"])[0].T
    cvec[:, 40:72] = f(inp["conv_w"])[0].reshape(4, 8, 128).transpose(2, 1, 0).reshape(128, 32)
    cvec[:, 72] = np.tile(f(inp["q_norm_gain"])[0], 2)
    cvec[:, 73] = np.tile(f(inp["k_norm_gain"])[0], 2)
    cvec[:, 74] = f(inp["subln_gain"])[0]
    lamv = np.zeros((128, 256), np.float32)
    for i, k in enumerate(("lambda_q1", "lambda_k1", "lambda_q2", "lambda_k2")):
        lamv[:, i * 64:(i + 1) * 64] = f(inp[k])[0][None, :]
    e1, cst = _consts()
    return {
        "w_in": f(inp["w_in"])[0],
        "w_out": f(inp["w_out"])[0],
        "cvec": cvec,
        "lamv": lamv,
        "relb": f(inp["rel_bias"]),
        "e1": e1,
        "cst": cst,
        "w_rg": np.ascontiguousarray(f(inp["w_rg"])[0].transpose(1, 0, 2)),
        "w_ig": np.ascontiguousarray(f(inp["w_ig"])[0].transpose(1, 0, 2)),
    }


def kernel(**inputs):
    shared = _prep_shared(inputs)
    x = np.asarray(inputs["x"], dtype=np.float32)
    if "nc" not in _NC_CACHE:
        _NC_CACHE["nc"] = build()
    nc = _NC_CACHE["nc"]
    in_maps = []
    for b in range(8):
        m = dict(shared)
        m["x"] = np.ascontiguousarray(x[b])
        in_maps.append(m)
    res = run_bass_kernel_spmd(nc, in_maps, core_ids=list(range(8)))
    return np.stack([np.asarray(r["out"], dtype=np.float32) for r in res.results], axis=0)
```

```python
import contextlib
import math

import numpy as np
import ml_dtypes
import concourse.bass as bass
import concourse.mybir as mybir
from concourse.bass_utils import run_bass_kernel_spmd

F32 = mybir.dt.float32
BF16 = mybir.dt.bfloat16
AF = mybir.ActivationFunctionType
ALU = mybir.AluOpType
AX = mybir.AxisListType

S = 4096
D = 1024
DIN = 6144
OFF_LX, OFF_LG, OFF_Q, OFF_K, OFF_V, OFF_AG = 0, 1024, 2048, 3072, 4096, 5120
EPS = 1e-6
LAM_INIT = 0.8 - 0.6 * math.exp(-0.3 * 0)
NCV = 80
MASKV = -30000.0
SAME_ENG_SYNC = True


class _Op:
    __slots__ = ("eng", "fn", "deps", "signal", "value", "dom", "ndma", "idx")


class Sched:
    ENGS = ("pe", "act", "dve", "pool", "sp")

    def __init__(self):
        self.q = {e: [] for e in self.ENGS}
        self.lastw = {}
        self.lastr = {}
        self.domops = {}
        self.bar = {}

    def op(self, eng, fn, reads=(), writes=(), dma=None, ndma=1):
        o = _Op()
        o.eng = eng
        o.fn = fn
        o.signal = False
        o.value = 0
        o.ndma = ndma
        o.dom = ("dma", dma) if dma is not None else eng
        deps = dict(self.bar)

        def add(d):
            if d is None:
                return
            k = d.dom
            if k not in deps or deps[k].idx < d.idx:
                deps[k] = d

        for r in reads:
            add(self.lastw.get(r))
        for r in writes:
            add(self.lastw.get(r))
            for d in self.lastr.get(r, {}).values():
                add(d)
        lst = self.domops.setdefault(o.dom, [])
        o.idx = len(lst)
        lst.append(o)
        for r in writes:
            self.lastw[r] = o
            self.lastr[r] = {}
        for r in reads:
            self.lastr.setdefault(r, {})[o.dom] = o
        o.deps = list(deps.values())
        self.q[eng].append(o)
        return o

    def barrier(self):
        self.bar = {dom: lst[-1] for dom, lst in self.domops.items() if lst}

    @staticmethod
    def _skip(o, d):
        return d.dom == o.dom and not isinstance(d.dom, tuple) and (o.eng == "pe" or not SAME_ENG_SYNC)

    def finalize(self):
        for ops in self.q.values():
            for o in ops:
                for d in o.deps:
                    if not self._skip(o, d):
                        d.signal = True
        for dom, lst in self.domops.items():
            c = 0
            for o in lst:
                if isinstance(dom, tuple):
                    c += 16 * o.ndma
                    o.value = c
                elif o.signal:
                    c += 1
                    o.value = c

    def emit(self, name, eng, sems):
        waited = {}
        for o in self.q[name]:
            for d in o.deps:
                if self._skip(o, d):
                    continue
                if waited.get(d.dom, 0) >= d.value:
                    continue
                eng.wait_ge(sems[d.dom], d.value)
                waited[d.dom] = d.value
            r = o.fn(eng)
            if isinstance(o.dom, tuple):
                for i in r:
                    i.then_inc(sems[o.dom], 16)
            elif o.signal:
                r.then_inc(sems[o.dom], 1)


def build(stop=None, dbg=False):
    nc = bass.Bass("TRN2", target_bir_lowering=False)

    def dram(name, shape, dtype, kind):
        return nc.dram_tensor(name, shape, dtype, kind=kind).ap()

    x_d = dram("x", [S, D], F32, "ExternalInput")
    win_d = dram("w_in", [D, DIN], F32, "ExternalInput")
    wout_d = dram("w_out", [2048, D], F32, "ExternalInput")
    cvec_d = dram("cvec", [128, NCV], F32, "ExternalInput")
    lamv_d = dram("lamv", [128, 256], F32, "ExternalInput")
    relb_d = dram("relb", [32, 8], F32, "ExternalInput")
    e1_d = dram("e1", [33, 384], F32, "ExternalInput")
    cst_d = dram("cst", [128, 256], F32, "ExternalInput")
    wrg_d = dram("w_rg", [128, 8, 128], F32, "ExternalInput")
    wig_d = dram("w_ig", [128, 8, 128], F32, "ExternalInput")
    out_d = dram("out", [S, D], F32, "ExternalOutput")
    mix_d = dram("mixscr", [16, 128, S], BF16, "ExternalOutput" if dbg else "Internal")
    vrow_d = dram("vrow", [8, 384], F32, "Internal")
    dbg_d = None

    SB0 = 16512
    SBTOP = 229344
    cur = [SB0]
    nid = [0]

    def sb(shape, dtype, at=None):
        nbytes = int(np.prod(shape[1:])) * (4 if dtype == F32 else 2)
        nbytes = (nbytes + 63) // 64 * 64
        if at is None:
            off = cur[0]
            cur[0] += nbytes
        else:
            off = at[0]
            at[0] += nbytes
        assert off + nbytes <= SBTOP, ("SBUF overflow", off + nbytes - SBTOP)
        nid[0] += 1
        return nc.alloc_sbuf_tensor_at("t%d" % nid[0], list(shape), dtype, offset=off)

    ps = nc.alloc_psum_tensor("ps", [128, 8, 512], F32)

    xT = sb([128, 8, S], BF16)
    cst = sb([128, 256], F32)
    ident = cst[:, 0:128]
    jmat = cst[:, 128:256]
    ones_f = sb([128, 128], F32)
    bones = sb([128, 128], BF16)
    cvec = sb([128, NCV], F32)
    lamv = sb([128, 256], F32)
    small = sb([128, 160], F32)
    bhi = sb([128, 8, 256], BF16)
    blo = sb([128, 8, 256], BF16)
    identb = sb([128, 128], BF16)
    PH0 = cur[0]
    PHTOP = (SBTOP // 64) * 64 - 8192 - 6144
    at_top = [PHTOP]
    wh0_top = sb([128, 3, 8, 128], BF16, at_top)
    wv = sb([128, 8, 512], BF16, at_top)
    wh = [wh0_top, None]

    def load_wh(h):
        srcs = [OFF_Q + h * 128, OFF_K + h * 128, OFF_AG + h * 128]
        op("pool", lambda e, h=h, srcs=srcs: [
            e.dma_start(out=wh[h % 2][:, i, :, :], in_=win_d[:, srcs[i]:srcs[i] + 128].rearrange("(k p) n -> p k n", p=128)) for i in range(3)],
           writes=[("wh", h % 2)], dma="wh%d" % (h % 2), ndma=3)

    def load_wv(g):
        op("pool", lambda e, g=g: [e.dma_start(out=wv[:], in_=win_d[:, OFF_V + g * 512:OFF_V + (g + 1) * 512].rearrange("(k p) n -> p k n", p=128))],
           writes=["wv"], dma="wv")

    CV_NG, CV_CB, CV_LL, CV_BR, CV_BI, CV_CW, CV_QG, CV_KG, CV_SG = 0, 8, 16, 24, 32, 40, 72, 73, 74
    SM_SS, SM_RSTD, SM_HBR, SM_HBI, SM_CC, SM_HC, SM_GQ8, SM_NLAM, SM_T0, SM_LS = 0, 32, 64, 72, 80, 88, 96, 97, 100, 120

    sch = Sched()
    op = sch.op

    def col(t, c):
        return t[:, c:c + 1]

    op("sp", lambda e: [e.dma_start(out=cst[:], in_=cst_d)], writes=["cst"], dma="cst")
    op("sp", lambda e: [e.dma_start(out=cvec[:], in_=cvec_d)], writes=["cvec"], dma="cvec")
    op("sp", lambda e: [e.dma_start(out=lamv[:], in_=lamv_d)], writes=["lamv"], dma="lamv")
    op("dve", lambda e: e.memset(ones_f[:], 1.0), writes=["ones_f"])
    op("dve", lambda e: e.memset(bones[:], 0.0), writes=["bones"])
    op("dve", lambda e: e.memset(bones[0:64, 0:64], 1.0), writes=["bones"])
    op("dve", lambda e: e.memset(bones[64:128, 64:128], 1.0), writes=["bones"])

    at = [PH0]
    rbaug = sb([33, 8], F32, at)
    e1s = sb([33, 384], F32, at)
    vsb = sb([8, 384], F32, at)
    hk8 = sb([128, 8, 256], F32, at)
    lprod = sb([128, 128], F32, at)
    biasT = sb([128, 8, 256], F32, at)
    btmp = sb([128, 8, 256], F32, at)
    PHA = at[0]

    op("dve", lambda e: e.memset(rbaug[32:33, :], MASKV), writes=["rbaug"])
    op("sp", lambda e: [e.dma_start(out=rbaug[0:32, :], in_=relb_d)], writes=["rbaug"], dma="rbaug")
    op("sp", lambda e: [e.dma_start(out=e1s[:], in_=e1_d)], writes=["e1s"], dma="e1s")
    op("pe", lambda e: e.matmul(ps[0:8, 0, 0:384], rbaug[:, :], e1s[:, :], start=True, stop=True),
       reads=["rbaug", "e1s"], writes=[("ps", 0)])
    op("dve", lambda e: e.tensor_copy(out=vsb[:], in_=ps[0:8, 0, 0:384]), reads=[("ps", 0)], writes=["vsb"])
    op("sp", lambda e: [e.dma_start(out=vrow_d, in_=vsb[:])], reads=["vsb"], writes=["vrow_d"], dma="vrow")
    op("sp", lambda e: [e.dma_start(out=hk8[:], in_=bass.AP(vrow_d.tensor, 0, [[1, 128], [384, 8], [1, 256]]))],
       reads=["vrow_d"], writes=["hk8"], dma="hk8")
    for h in range(8):
        b = h % 4
        op("pe", lambda e, h=h, b=b: e.matmul(ps[:, b, 0:256], jmat, hk8[:, h, :], start=True, stop=True),
           reads=["hk8", "cst"], writes=[("ps", b)])
        op("dve", lambda e, h=h, b=b: e.tensor_copy(out=biasT[:, h, :], in_=ps[:, b, 0:256]),
           reads=[("ps", b)], writes=["biasT"])

    op("dve", lambda e: e.tensor_copy(out=bhi[:], in_=biasT[:]), reads=["biasT"], writes=["bhi"])
    op("dve", lambda e: e.tensor_tensor(out=btmp[:], in0=biasT[:], in1=bhi[:], op=ALU.subtract), reads=["biasT", "bhi"], writes=["btmp"])
    op("dve", lambda e: e.tensor_copy(out=blo[:], in_=btmp[:]), reads=["btmp"], writes=["blo"])
    op("dve", lambda e: e.tensor_copy(out=identb[:], in_=ident), reads=["cst"], writes=["identb"])
    op("dve", lambda e: e.tensor_tensor(out=lprod[:, 0:64], in0=lamv[:, 0:64], in1=lamv[:, 64:128], op=ALU.mult),
       reads=["lamv"], writes=["lprod"])
    op("dve", lambda e: e.tensor_tensor(out=lprod[:, 64:128], in0=lamv[:, 128:192], in1=lamv[:, 192:256], op=ALU.mult),
       reads=["lamv"], writes=["lprod"])
    op("dve", lambda e: e.tensor_reduce(out=small[:, SM_LS:SM_LS + 2], in_=lprod[:].rearrange("p (a b) -> p a b", a=2),
                                        axis=AX.X, op=ALU.add), reads=["lprod"], writes=["sm_ls"])
    op("act", lambda e: e.activation(out=small[:, SM_LS + 2:SM_LS + 4], in_=small[:, SM_LS:SM_LS + 2], func=AF.Exp),
       reads=["sm_ls"], writes=["sm_le"])
    op("dve", lambda e: e.tensor_tensor(out=col(small, SM_LS + 4), in0=col(small, SM_LS + 2), in1=col(small, SM_LS + 3),
                                        op=ALU.subtract), reads=["sm_le"], writes=["sm_l"])
    op("dve", lambda e: e.tensor_scalar(out=col(small, SM_NLAM), in0=col(small, SM_LS + 4), scalar1=LAM_INIT, scalar2=-1.0,
                                        op0=ALU.add, op1=ALU.mult), reads=["sm_l"], writes=["nlam"])
    op("act", lambda e: e.activation(out=small[:, SM_T0:SM_T0 + 8], in_=cvec[:, CV_LL:CV_LL + 8], func=AF.Exp, scale=-1.0),
       reads=["cvec"], writes=["sm_t0"])
    op("act", lambda e: e.activation(out=small[:, SM_T0 + 8:SM_T0 + 16], in_=small[:, SM_T0:SM_T0 + 8], func=AF.Ln, bias=1.0),
       reads=["sm_t0"], writes=["sm_t1"])
    op("dve", lambda e: e.tensor_scalar(out=small[:, SM_CC:SM_CC + 8], in0=small[:, SM_T0 + 8:SM_T0 + 16], scalar1=-8.0,
                                        scalar2=None, op0=ALU.mult), reads=["sm_t1"], writes=["lruc"])
    op("dve", lambda e: e.tensor_scalar(out=small[:, SM_HC:SM_HC + 8], in0=small[:, SM_T0 + 8:SM_T0 + 16], scalar1=-4.0,
                                        scalar2=None, op0=ALU.mult), reads=["sm_t1"], writes=["lruc"])
    op("dve", lambda e: e.tensor_scalar(out=small[:, SM_HBR:SM_HBR + 16], in0=cvec[:, CV_BR:CV_BR + 16], scalar1=0.5,
                                        scalar2=None, op0=ALU.mult), reads=["cvec"], writes=["lruc"])
    op("dve", lambda e: e.tensor_scalar(out=col(small, SM_GQ8), in0=col(cvec, CV_QG), scalar1=0.125, scalar2=None,
                                        op0=ALU.mult), reads=["cvec"], writes=["gq8"])

    at = [PHA]
    xa = [sb([128, 4, D], F32, at) for _ in range(3)]
    sqj = sb([128, D], BF16, at)
    dg = [sb([128, 128], F32, at) for _ in range(2)]
    lnt0 = sb([128, 32], F32, at)

    for tg in range(8):
        buf = xa[tg % 3]
        bk = [("xa", tg % 3, j) for j in range(4)]
        op("sp", lambda e, tg=tg, buf=buf: [e.dma_start(
            out=buf[:], in_=x_d[tg * 512:(tg + 1) * 512, :].rearrange("(j p) d -> p j d", p=128))],
           writes=bk, dma="xa%d" % (tg % 3))
        for j in range(4):
            tb = tg * 4 + j
            op("act", lambda e, buf=buf, j=j, tb=tb: e.activation(out=sqj[:], in_=buf[:, j, :], func=AF.Square,
                                                                  accum_out=col(small, SM_SS + tb)),
               reads=[bk[j]], writes=["sqj", ("ss", tg)])
        op("act", lambda e, tg=tg: e.activation(out=lnt0[:, tg * 4:tg * 4 + 4], in_=small[:, SM_SS + tg * 4:SM_SS + tg * 4 + 4],
                                               func=AF.Ln, scale=1.0 / D, bias=EPS), reads=[("ss", tg)], writes=[("lnt0", tg)])
        op("act", lambda e, tg=tg: e.activation(out=small[:, SM_RSTD + tg * 4:SM_RSTD + tg * 4 + 4], in_=lnt0[:, tg * 4:tg * 4 + 4],
                                               func=AF.Exp, scale=-0.5), reads=[("lnt0", tg)], writes=[("rstd", tg)])
        for j in range(4):
            tb = tg * 4 + j
            if j < 3:
                op("act", lambda e, buf=buf, j=j, tb=tb: e.activation(out=buf[:, j, :], in_=buf[:, j, :], func=AF.Copy,
                                                                      scale=col(small, SM_RSTD + tb)),
                   reads=[bk[j], ("rstd", tg)], writes=[bk[j]])
            else:
                op("dve", lambda e, buf=buf, j=j, tb=tb: e.tensor_scalar(out=buf[:, j, :], in0=buf[:, j, :], scalar1=col(small, SM_RSTD + tb),
                                                                         scalar2=1.0, op0=ALU.mult, op1=ALU.mult),
                   reads=[bk[j], ("rstd", tg)], writes=[bk[j]])
        for c in range(8):
            def tr(e, buf=buf, c=c):
                r = None
                for j in range(4):
                    r = e.transpose(ps[:, c, j * 128:(j + 1) * 128], buf[:, j, c * 128:(c + 1) * 128], ident)
                return r
            op("pe", tr, reads=bk + ["cst"], writes=[("ps", c)])
            op("dve", lambda e, c=c, tg=tg: e.tensor_scalar(out=xT[:, c, tg * 512:(tg + 1) * 512], in0=ps[:, c, :],
                                                            scalar1=col(cvec, CV_NG + c), scalar2=None, op0=ALU.mult),
               reads=[("ps", c), "cvec"], writes=[("xT", c, tg)])
    sch.barrier()

    if stop == "A":
        return _finish(nc, sch, dbg_d, None, ps, None)

    def proj_fm(wt, widx, tc, bank, wres):
        def f(e):
            r = None
            for k in range(8):
                r = e.matmul(ps[:, bank, :], wt[:, widx, k, :], xT[:, k, tc * 512:(tc + 1) * 512], start=(k == 0), stop=(k == 7))
            return r
        op("pe", f, reads=[wres] + [("xT", k, tc) for k in range(8)], writes=[("ps", bank)])

    bankc = [0]

    def nbank(n=8, lo=0):
        b = lo + bankc[0] % n
        bankc[0] += 1
        return b

    TL = 1024
    NSET = 2
    NQ = TL // 512
    NS = S // TL
    at = [PH0]
    wl = [sb([128, 2, 8, 128], BF16, at) for _ in range(2)]
    wrg = sb([128, 8, 128], BF16, at)
    wig = sb([128, 8, 128], BF16, at)
    dgw = [sb([128, 4, 128], BF16, at) for _ in range(2)]
    hcar = sb([128, 8], F32, at)

    def lset():
        return dict(xlb=sb([128, TL + 16], BF16, at), xlb2=sb([128, TL + 16], BF16, at), gl=sb([128, TL], F32, at), xc=sb([128, TL], F32, at),
                    xcb=sb([128, TL], BF16, at), rr=sb([128, TL], F32, at), ti=sb([128, TL], F32, at),
                    aa=sb([128, TL], F32, at), a2=sb([128, TL], F32, at), uu=sb([128, TL], F32, at),
                    hh=sb([128, TL], F32, at), tg=sb([128, TL], F32, at), ylb=sb([128, TL], BF16, at))
    LS = [lset() for _ in range(NSET)]
    assert at[0] <= PHTOP, (at[0], PHTOP)

    op("pool", lambda e: [e.dma_start(out=wrg[:], in_=wrg_d)], writes=["wrg"], dma="wrg")
    op("pool", lambda e: [e.dma_start(out=wig[:], in_=wig_d)], writes=["wig"], dma="wig")

    def load_wl(c):
        w = wl[c % 2]
        op("pool", lambda e, c=c, w=w: [
            e.dma_start(out=w[:, 0, :, :], in_=win_d[:, OFF_LX + c * 128:OFF_LX + (c + 1) * 128].rearrange("(k p) n -> p k n", p=128)),
            e.dma_start(out=w[:, 1, :, :], in_=win_d[:, OFF_LG + c * 128:OFF_LG + (c + 1) * 128].rearrange("(k p) n -> p k n", p=128)),
        ], writes=[("wl", c % 2)], dma="wl%d" % (c % 2), ndma=2)

    def lru_front(u):
        c, s_ = divmod(u, NS)
        p = u % NSET
        po = (u - 1) % NSET
        B = LS[p]
        w = wl[c % 2]
        wres = ("wl", c % 2)
        R = lambda n: ("L", n, p)
        if s_ == 0:
            if c + 1 < 8:
                load_wl(c + 1)
            for j in range(4):
                op("dve", lambda e, j=j, c=c: e.tensor_scalar(out=dgw[c % 2][:, j, :], in0=ident, scalar1=col(cvec, CV_CW + c * 4 + j),
                                                              scalar2=None, op0=ALU.mult), reads=["cst", "cvec"], writes=[("dgw", c % 2)])
            op("dve", lambda e: e.memset(B["xlb"][:, 4:8], 0.0), writes=[R("xlb")])
        else:
            Bo = LS[po]
            op("dve", lambda e: e.tensor_copy(out=B["xlb"][:, 4:8], in_=Bo["xlb"][:, TL + 4:TL + 8]), reads=[("L", "xlb", po)], writes=[R("xlb")])
        for q in range(NQ):
            tc = s_ * NQ + q
            sl = slice(q * 512, (q + 1) * 512)
            b1 = nbank()
            proj_fm(w, 0, tc, b1, wres)
            op("dve", lambda e, q=q, b1=b1: e.tensor_copy(out=B["xlb"][:, 8 + q * 512:8 + (q + 1) * 512], in_=ps[:, b1, :]),
               reads=[("ps", b1)], writes=[R("xlb")])
            b2 = nbank()
            proj_fm(w, 1, tc, b2, wres)
            op("act", lambda e, sl=sl, b2=b2: e.activation(out=B["gl"][:, sl], in_=ps[:, b2, :], func=AF.Copy),
               reads=[("ps", b2)], writes=[R("gl")])
        op("dve", lambda e: e.tensor_copy(out=B["xlb2"][:, 5:TL + 9], in_=B["xlb"][:, 4:TL + 8]), reads=[R("xlb")], writes=[R("xlb2")])
        taps = [("xlb2", 6), ("xlb", 6), ("xlb2", 8), ("xlb", 8)]
        for q in range(NQ):
            sl = slice(q * 512, (q + 1) * 512)
            b3 = nbank()

            def fcv(e, q=q, b3=b3, c=c):
                r = None
                for j in range(4):
                    nm, o_ = taps[j]
                    r = e.matmul(ps[:, b3, :], dgw[c % 2][:, j, :], B[nm][:, o_ + q * 512:o_ + (q + 1) * 512],
                                 start=(j == 0), stop=(j == 3))
                return r
            op("pe", fcv, reads=[R("xlb"), R("xlb2"), ("dgw", c % 2)], writes=[("ps", b3)])
            op("dve", lambda e, sl=sl, b3=b3, c=c: e.tensor_scalar(out=B["xcb"][:, sl], in0=ps[:, b3, :], scalar1=col(cvec, CV_CB + c),
                                                                   scalar2=None, op0=ALU.add), reads=[("ps", b3), "cvec"], writes=[R("xcb")])
        for q in range(NQ):
            sl = slice(q * 512, (q + 1) * 512)
            b1 = nbank()
            op("pe", lambda e, c=c, sl=sl, b1=b1: e.matmul(ps[:, b1, :], wrg[:, c, :], B["xcb"][:, sl], start=True, stop=True),
               reads=["wrg", R("xcb")], writes=[("ps", b1)])
            op("act", lambda e, c=c, sl=sl, b1=b1: e.activation(out=B["rr"][:, sl], in_=ps[:, b1, :], func=AF.Tanh, scale=0.5,
                                                                bias=col(small, SM_HBR + c)), reads=[("ps", b1), "lruc"], writes=[R("rr")])
            b2 = nbank()
            op("pe", lambda e, c=c, sl=sl, b2=b2: e.matmul(ps[:, b2, :], wig[:, c, :], B["xcb"][:, sl], start=True, stop=True),
               reads=["wig", R("xcb")], writes=[("ps", b2)])
            op("act", lambda e, c=c, sl=sl, b2=b2: e.activation(out=B["ti"][:, sl], in_=ps[:, b2, :], func=AF.Tanh, scale=0.5,
                                                                bias=col(small, SM_HBI + c)), reads=[("ps", b2), "lruc"], writes=[R("ti")])
        op("act", lambda e: e.activation(out=B["tg"][:], in_=B["gl"][:], func=AF.Silu), reads=[R("gl")], writes=[R("tg")])
        op("act", lambda e, c=c: e.activation(out=B["aa"][:], in_=B["rr"][:], func=AF.Exp, scale=col(small, SM_HC + c), bias=col(small, SM_HC + c)),
           reads=[R("rr"), "lruc"], writes=[R("aa")])
        op("act", lambda e, c=c: e.activation(out=B["a2"][:], in_=B["rr"][:], func=AF.Exp, scale=col(small, SM_CC + c), bias=col(small, SM_CC + c)),
           reads=[R("rr"), "lruc"], writes=[R("a2")])
        op("act", lambda e: e.activation(out=B["a2"][:], in_=B["a2"][:], func=AF.Ln, scale=-1.0, bias=1.0), reads=[R("a2")], writes=[R("a2")])
        op("act", lambda e: e.activation(out=B["a2"][:], in_=B["a2"][:], func=AF.Exp, scale=0.5, bias=math.log(0.5)), reads=[R("a2")], writes=[R("a2")])

    def lru_back(u):
        c, s_ = divmod(u, NS)
        p = u % NSET
        B = LS[p]
        R = lambda n: ("L", n, p)
        t0 = s_ * TL
        op("dve", lambda e: e.scalar_tensor_tensor(out=B["uu"][:], in0=B["ti"][:], scalar=1.0, in1=B["xcb"][:], op0=ALU.add, op1=ALU.mult),
           reads=[R("ti"), R("xcb")], writes=[R("uu")])
        op("dve", lambda e: e.tensor_tensor(out=B["uu"][:], in0=B["uu"][:], in1=B["a2"][:], op=ALU.mult), reads=[R("uu"), R("a2")], writes=[R("uu")])
        if s_ == 0:
            op("dve", lambda e: e.tensor_tensor_scan(out=B["hh"][:], data0=B["aa"][:], data1=B["uu"][:], initial=0.0, op0=ALU.mult, op1=ALU.add),
               reads=[R("aa"), R("uu")], writes=[R("hh")])
        else:
            Bp = LS[(u - 1) % NSET]
            op("dve", lambda e: e.tensor_tensor_scan(out=B["hh"][:], data0=B["aa"][:], data1=B["uu"][:], initial=Bp["hh"][:, TL - 1:TL],
                                                     op0=ALU.mult, op1=ALU.add),
               reads=[R("aa"), R("uu"), ("L", "hh", (u - 1) % NSET)], writes=[R("hh")])
        op("dve", lambda e: e.tensor_tensor(out=B["ylb"][:], in0=B["hh"][:], in1=B["tg"][:], op=ALU.mult),
           reads=[R("hh"), R("tg")], writes=[R("ylb")])
        op("sp", lambda e, c=c, t0=t0: [e.dma_start(out=mix_d[c, :, t0:t0 + TL], in_=B["ylb"][:])],
           reads=[R("ylb")], writes=[("mixd", c)], dma="ylb%d" % p)

    load_wl(0)
    NU = 8 * NS
    for u in range(NU + NSET - 1):
        if u < NU:
            lru_front(u)
        if u >= NSET - 1:
            lru_back(u - (NSET - 1))
    load_wh(0)
    load_wv(0)
    sch.barrier()

    if stop == "B":
        return _finish(nc, sch, dbg_d, None, ps, None)

    at = [PH0]
    Vc = sb([128, 32, 4, 130], BF16, at)
    qT = sb([128, S], BF16, at)
    kT = sb([128, S], BF16, at)
    sgT = sb([128, S], BF16, at)
    wh[1] = sb([128, 3, 8, 128], BF16, at)
    NZ = 4
    NB_IP = 7
    zsq = [sb([128, 512], BF16, at) for _ in range(NZ)]
    rq = [sb([128, 512], F32, at) for _ in range(NZ)]
    NPT = 4
    pt = [sb([128, 2, 512], BF16, at) for _ in range(NPT)]
    accs2 = [sb([128, 8 * 129], F32, at) for _ in range(2)]
    rden2 = [sb([128, 8], F32, at) for _ in range(2)]
    ot2 = [sb([128, 4, 128], F32, at) for _ in range(2)]
    junk = sb([128, 128], F32, at)
    ssq2_ = [sb([128, 8], F32, at) for _ in range(2)]
    yn2 = [sb([128, 4, 128], BF16, at) for _ in range(2)]
    ps7b = ps[:, 7, :].bitcast(BF16)
    yat = [sb([128, 512], BF16, at) for _ in range(2)]
    assert at[0] <= PHTOP, (at[0], PHTOP)

    def _addr(t):
        return t[:].tensor.manual_sbuf_range[0]
    assert _addr(zsq[0]) == _addr(wh[1]) + 6144 and _addr(rq[3]) == _addr(rq[0]) + 3 * 2048 and _addr(pt[3]) == _addr(pt[0]) + 3 * 2048
    wo_q = [nc.alloc_sbuf_tensor_at("woq%d" % i, [128, 4, D], BF16, offset=a_)
            for i, a_ in enumerate((_addr(wh[1]), _addr(rq[0]), _addr(wv), _addr(pt[0])))]
    wo_res = [[("wh", 1), ("zsq", 0), ("zsq", 1)], [("rq", i) for i in range(NZ)], ["wv"], [("pt", i) for i in range(NPT)]]

    def load_wo(wq):
        op("pool", lambda e: [e.dma_start(out=wo_q[wq][:], in_=wout_d[512 * wq:512 * (wq + 1), :].rearrange("(k p) n -> p k n", p=128))],
           writes=[("wo", wq)] + wo_res[wq], dma="wo%d" % wq)

    op("dve", lambda e: e.memset(Vc[:, :, :, 128:130], 1.0), writes=["Vc"])

    zi = [0]
    carry = []
    yidx = 0
    for h in range(8):
        if h + 1 < 8:
            load_wh(h + 1)
        w = wh[h % 2]
        wres = ("wh", h % 2)
        vgroups = []
        if h == 1:
            load_wv(1)
        if h % 4 == 0:

            def vgroup(tb):
                bk_ = nbank(NB_IP)

                def fv(e):
                    r = None
                    for k in range(8):
                        r = e.matmul(ps[:, bk_, :], xT[:, k, tb * 128:(tb + 1) * 128], wv[:, k, :], start=(k == 0), stop=(k == 7))
                    return r
                op("pe", fv, reads=["wv"] + [("xT", k, tb // 4) for k in range(8)], writes=[("ps", bk_)])
                op("dve", lambda e: e.tensor_copy(out=Vc[:, tb, :, 0:128], in_=ps[:, bk_, :].rearrange("p (a b) -> p a b", a=4)),
                   reads=[("ps", bk_)], writes=["Vc"])
            for tb_ in range(32):
                vgroup(tb_)
        units = [(which, tc) for which in (0, 1) for tc in range(8)]
        st = {}

        def ip1(which, tc):
            i2 = zi[0] % NZ
            zi[0] += 1
            b1 = nbank(NB_IP)
            st[(which, tc)] = (i2, b1)
            proj_fm(w, which, tc, b1, wres)
            op("act", lambda e: e.activation(out=zsq[i2][:], in_=ps[:, b1, :], func=AF.Square),
               reads=[("ps", b1)], writes=[("zsq", i2)])

        def ip2(which, tc):
            i2, b1 = st[(which, tc)]
            dst = qT if which == 0 else kT
            b2 = nbank(NB_IP)
            op("pe", lambda e: e.matmul(ps[:, b2, :], bones[:], zsq[i2][:], start=True, stop=True),
               reads=["bones", ("zsq", i2)], writes=[("ps", b2)])
            op("act", lambda e: e.activation(out=rq[i2][:], in_=ps[:, b2, :], func=AF.Ln, scale=1.0 / 64, bias=EPS),
               reads=[("ps", b2)], writes=[("rq", i2)])
            op("act", lambda e: e.activation(out=rq[i2][:], in_=rq[i2][:], func=AF.Exp, scale=-0.5),
               reads=[("rq", i2)], writes=[("rq", i2)])
            gsc = col(small, SM_GQ8) if which == 0 else col(cvec, CV_KG)
            op("dve", lambda e: e.scalar_tensor_tensor(
                out=dst[:, tc * 512:(tc + 1) * 512], in0=ps[:, b1, :], scalar=gsc, in1=rq[i2][:], op0=ALU.mult, op1=ALU.mult),
               reads=[("ps", b1), ("rq", i2), "gq8", "cvec"], writes=[("qk", which, tc)])

        for ui in range(len(units) + 1):
            if ui < len(units):
                ip1(*units[ui])
            if ui >= 1:
                ip2(*units[ui - 1])
            if ui >= 3 and carry:
                carry.pop(0)()
        while carry:
            carry.pop(0)()
        while vgroups:
            vgroup(vgroups.pop(0))
        for tc in range(8):
            i2 = zi[0] % NZ
            zi[0] += 1
            b1 = nbank(NB_IP)
            proj_fm(w, 2, tc, b1, wres)
            op("act", lambda e, tc=tc, b1=b1: e.activation(out=sgT[:, tc * 512:(tc + 1) * 512], in_=ps[:, b1, :], func=AF.Silu),
               reads=[("ps", b1)], writes=[("sg", tc)])
        if h == 7:
            for wq_ in range(3):
                load_wo(wq_)
        hv = h % 4
        QORDER = (0, 1, 2, 3, 4, 5, 6, 7)
        steps = [(qc, kb) for qc in QORDER for kb in range(4 * qc + 4)]
        nst = len(steps)
        deferred = []

        def s_qk(i, h=h):
            qc, kb = steps[i]
            j0 = max(0, kb - 4 * qc)
            c0 = j0 * 128
            sb_ = (i % 2) * 2
            sres = [("ps", sb_), ("ps", sb_ + 1)]

            near = kb >= 4 * qc - 1
            if near:
                if kb == 4 * qc - 1:
                    cc0, wdt, bc0 = 0, 128, 128
                else:
                    cc0, wdt, bc0 = c0, min(256, 512 - c0), 0

            def fqk(e):
                e.matmul(ps[:, sb_, c0:512], kT[0:64, kb * 128:(kb + 1) * 128], qT[0:64, qc * 512 + c0:(qc + 1) * 512],
                         start=True, stop=not near, skip_group_check=True)
                r = e.matmul(ps[:, sb_ + 1, c0:512], kT[64:128, kb * 128:(kb + 1) * 128],
                             qT[64:128, qc * 512 + c0:(qc + 1) * 512], start=True, stop=not near, skip_group_check=True)
                if near:
                    for comp in range(2):
                        e.matmul(ps[:, sb_ + comp, cc0:cc0 + wdt], identb[:], bhi[:, h, bc0:bc0 + wdt], start=False, stop=False,
                                 skip_group_check=True)
                        r = e.matmul(ps[:, sb_ + comp, cc0:cc0 + wdt], identb[:], blo[:, h, bc0:bc0 + wdt], start=False, stop=True,
                                     skip_group_check=True)
                return r
            op("pe", fqk, reads=[("qk", 0, qc), ("qk", 1, kb // 4), "bhi", "blo", "identb"], writes=sres)
            p_ = pt[i % NPT]
            op("act", lambda e: e.activation(out=p_[:, :, c0:512], in_=ps[:, sb_:sb_ + 2, c0:512], func=AF.Exp),
               reads=sres, writes=[("pt", i % NPT)])

        def s_pv(i, hv=hv):
            qc, kb = steps[i]
            j0 = max(0, kb - 4 * qc)
            p_ = pt[i % NPT]

            def fpv(e):
                r = None
                for j in range(j0, 4):
                    for comp in range(2):
                        a = j * 2 + comp
                        bank = 4 + a // 3
                        o0 = (a % 3) * 129
                        r = e.matmul(ps[:, bank, o0:o0 + 129], p_[:, comp, j * 128:(j + 1) * 128], Vc[:, kb, hv, 0:129],
                                     start=(kb == 0 and a % 3 == 0), stop=(kb == 4 * qc + j), skip_group_check=True)
                return r
            touched = sorted({4 + (j * 2 + comp) // 3 for j in range(j0, 4) for comp in range(2)})
            op("pe", fpv, reads=[("pt", i % NPT), "Vc"], writes=[("ps", b_) for b_ in touched])

        def acc_copy(bnk, ts_):
            accs = accs2[ts_]
            n = 3 if bnk < 2 else 2
            op("dve", lambda e: e.tensor_copy(out=accs[:, bnk * 387:bnk * 387 + n * 129], in_=ps[:, 4 + bnk, 0:n * 129]),
               reads=[("ps", 4 + bnk)], writes=[("accs", ts_)])

        def tail1(qc, ts_):
            accs, rden, ot, ssq = accs2[ts_], rden2[ts_], ot2[ts_], ssq2_[ts_]
            av = accs[:].rearrange("p (a b) -> p a b", a=8)
            rv = rden[:].rearrange("p (j c) -> p j c", c=2)
            op("dve", lambda e: e.reciprocal(out=rden[:], in_=av[:, :, 128]), reads=[("accs", ts_)], writes=[("rden", ts_)])
            op("dve", lambda e: e.tensor_scalar(out=rv[:, :, 1], in0=rv[:, :, 1], scalar1=col(small, SM_NLAM), scalar2=None,
                                                op0=ALU.mult), reads=[("rden", ts_), "nlam"], writes=[("rden", ts_)])
            for j in range(4):
                op("dve", lambda e, j=j: e.tensor_scalar(out=ot[:, j, :], in0=av[:, 2 * j, 0:128], scalar1=rden[:, 2 * j:2 * j + 1],
                                                         scalar2=None, op0=ALU.mult), reads=[("accs", ts_), ("rden", ts_)], writes=[("ot", ts_, j)])
                op("dve", lambda e, j=j: e.scalar_tensor_tensor(out=ot[:, j, :], in0=av[:, 2 * j + 1, 0:128],
                                                                scalar=rden[:, 2 * j + 1:2 * j + 2], in1=ot[:, j, :],
                                                                op0=ALU.mult, op1=ALU.add),
                   reads=[("accs", ts_), ("rden", ts_), ("ot", ts_, j)], writes=[("ot", ts_, j)])
                op("dve", lambda e, j=j: e.scalar_tensor_tensor(out=junk[:], in0=ot[:, j, :], scalar=1.0, in1=ot[:, j, :],
                                                                op0=ALU.mult, op1=ALU.mult, accum_out=ssq[:, j:j + 1]),
                   reads=[("ot", ts_, j)], writes=["junk", ("ssq", ts_, j)])

        def tail2(qc, ts_, h=h):
            ot, ssq, yn = ot2[ts_], ssq2_[ts_], yn2[ts_]
            op("act", lambda e: e.activation(out=ssq[:, 4:8], in_=ssq[:, 0:4], func=AF.Ln, scale=1.0 / 128, bias=EPS),
               reads=[("ssq", ts_, j) for j in range(4)], writes=[("ssqr", ts_)])
            op("act", lambda e: e.activation(out=ssq[:, 4:8], in_=ssq[:, 4:8], func=AF.Exp, scale=-0.5),
               reads=[("ssqr", ts_)], writes=[("ssqr", ts_)])
            for j in range(4):
                op("dve", lambda e, j=j: e.tensor_scalar(out=yn[:, j, :], in0=ot[:, j, :], scalar1=ssq[:, 4 + j:5 + j],
                                                         scalar2=(1.0 - LAM_INIT), op0=ALU.mult, op1=ALU.mult),
                   reads=[("ot", ts_, j), ("ssqr", ts_)], writes=[("yn", ts_)])

        def tail3(qc, ts_, h=h):
            nonlocal yidx
            yn = yn2[ts_]

            def ftr(e):
                r = None
                for j in range(4):
                    r = e.transpose(ps7b[:, j * 128:(j + 1) * 128], yn[:, j, :], identb[:])
                return r
            op("pe", ftr, reads=[("yn", ts_), "identb"], writes=[("ps", 7)])
            ya = yat[yidx % 2]
            yres = ("yat", yidx % 2)
            op("dve", lambda e: e.scalar_tensor_tensor(out=ya[:], in0=ps7b[:, 0:512], scalar=col(cvec, CV_SG),
                                                       in1=sgT[:, qc * 512:(qc + 1) * 512], op0=ALU.mult, op1=ALU.mult),
               reads=[("ps", 7), "cvec", ("sg", qc)], writes=[yres])
            op("sp", lambda e: [e.dma_start(out=mix_d[8 + h, :, qc * 512:(qc + 1) * 512], in_=ya[:])],
               reads=[yres], writes=[("mixd", 8 + h)], dma="yat%d" % (yidx % 2))
            yidx += 1

        TD = 5
        def la_of(s_):
            qc_, kb_ = steps[s_]
            return 3 if (kb_ == 0 and s_ > 0) else 2
        pvn = 0
        i = 0
        while pvn < nst:
            if i < nst:
                s_qk(i)
            for dd in [d_ for d_ in deferred if d_[0] <= i]:
                dd[1]()
                deferred.remove(dd)
            while pvn < nst and (pvn <= i - la_of(pvn) or i >= nst + 3):
                s_pv(pvn)
                qc_, kb_ = steps[pvn]
                tsi = QORDER.index(qc_) % 2
                if kb_ >= 4 * qc_ + 1:
                    acc_copy(kb_ - 4 * qc_ - 1, tsi)
                if kb_ == 4 * qc_ + 3:
                    tail1(qc_, tsi)
                    deferred.append((i + TD, lambda qc_=qc_, tsi=tsi, f_=tail2: f_(qc_, tsi)))
                    deferred.append((i + TD + 4, lambda qc_=qc_, tsi=tsi, f_=tail3: f_(qc_, tsi)))
                pvn += 1
            i += 1
        carry[:] = [dd[1] for dd in deferred]
        if h == 7:
            load_wo(3)
            for f_ in carry:
                f_()
    sch.barrier()

    if stop == "C":
        return _finish(nc, sch, dbg_d, None, ps, None)

    at = [PH0]
    mixc = [sb([128, 16, 512], BF16, at) for _ in range(2)]
    xres = [sb([128, D], F32, at) for _ in range(2)]
    ores = [sb([128, D], F32, at) for _ in range(2)]
    assert at[0] <= _addr(wh[1]), (at[0], _addr(wh[1]))
    last_stores = []
    for tc in range(8):
        m = mixc[tc % 2]
        mres = ("mixc", tc % 2)
        op("sp", lambda e, m=m, tc=tc: [e.dma_start(out=m[:], in_=mix_d[:, :, tc * 512:(tc + 1) * 512].rearrange("e p t -> p e t"))],
           reads=[("mixd", i) for i in range(16)], writes=[mres], dma="mixc%d" % (tc % 2))
        for j in range(4):
            tb = tc * 4 + j
            xr = xres[tb % 2]
            orr = ores[tb % 2]
            op("sp", lambda e, xr=xr, tb=tb: [e.dma_start(out=xr[:], in_=x_d[tb * 128:(tb + 1) * 128, :])],
               writes=[("xres", tb % 2)], dma="xres%d" % (tb % 2))
            for half in range(2):
                bk_ = nbank()

                for wq in range(4):
                    def fo(e, m=m, j=j, half=half, bk_=bk_, wq=wq):
                        r = None
                        for ei in range(4 * wq, 4 * wq + 4):
                            r = e.matmul(ps[:, bk_, :], m[:, ei, j * 128:(j + 1) * 128], wo_q[ei // 4][:, ei % 4, half * 512:(half + 1) * 512],
                                         start=(ei == 0), stop=(ei == 15))
                        return r
                    op("pe", fo, reads=[mres, ("wo", wq)], writes=[("ps", bk_)])
                op("dve", lambda e, xr=xr, orr=orr, half=half, bk_=bk_: e.tensor_tensor(
                    out=orr[:, half * 512:(half + 1) * 512], in0=ps[:, bk_, :], in1=xr[:, half * 512:(half + 1) * 512], op=ALU.add),
                   reads=[("ps", bk_), ("xres", tb % 2)], writes=[("ores", tb % 2)])
            st = op("act", lambda e, orr=orr, tb=tb: [e.dma_start(out=out_d[tb * 128:(tb + 1) * 128, :], in_=orr[:])],
                    reads=[("ores", tb % 2)], writes=[("outd", tb)], dma="ores%d" % (tb % 2))
            last_stores.append(st)
    return _finish(nc, sch, dbg_d, None, ps, None)


def _finish(nc, sch, dbg_d, dump, ps, _):
    sch.barrier()
    sch.op("sp", lambda e: None)
    sch.op("pe", lambda e: None)
    sch.op("act", lambda e: None)
    sch.op("dve", lambda e: None)
    sch.op("pool", lambda e: None)
    sch.finalize()
    with contextlib.ExitStack() as es:
        sems = {}
        for dom in sch.domops:
            nm = dom if isinstance(dom, str) else "d_" + dom[1]
            sems[dom] = es.enter_context(nc.semaphore(nm))
        blk = es.enter_context(nc.Block())

        @blk.sync
        def _(e):
            sch.emit("sp", e, sems)

        @blk.tensor
        def _(e):
            sch.emit("pe", e, sems)

        @blk.scalar
        def _(e):
            sch.emit("act", e, sems)

        @blk.vector
        def _(e):
            sch.emit("dve", e, sems)

        @blk.gpsimd
        def _(e):
            sch.emit("pool", e, sems)
    return nc


def _t5_bucket(n):
    n = np.asarray(n, dtype=np.int64)
    nf = np.maximum(n, 1).astype(np.float32)
    large = 16 + (np.log(nf / np.float32(16.0)) / np.float32(math.log(128 / 16)) * np.float32(16.0)).astype(np.int32)
    large = np.minimum(large, 31)
    return np.where(n < 16, n, large)


def _consts():
    e1 = np.zeros((33, 384), np.float32)
    for m in range(383):
        dist = m - 127
        if dist < 0:
            e1[32, m] = 1.0
        else:
            b = int(_t5_bucket(dist))
            e1[b, m] += 1.0
            e1[31, m] -= 1.0
    cst = np.zeros((128, 256), np.float32)
    cst[:, 0:128] = np.eye(128, dtype=np.float32)
    cst[:, 128:256] = np.eye(128, dtype=np.float32)[::-1]
    return e1, cst


_NC_CACHE = {}


def _prep_shared(inp):
    f = lambda a: np.ascontiguousarray(np.asarray(a, dtype=np.float32))
    cvec = np.zeros((128, NCV), np.float32)
    cvec[:, 0:8] = f(inp["norm_gain"])[0].reshape(8, 128).T
    cvec[:, 8:16] = f(inp["conv_b"])[0].reshape(8, 128).T
    cvec[:, 16:24] = f(inp["lru_lambda"])[0].reshape(8, 128).T
    cvec[:, 24:32] = f(inp["b_rg"])[0].T
    cvec[:, 32:40] = f(inp["b_ig"])[0].T
    cvec[:, 40:72] = f(inp["conv_w"])[0].reshape(4, 8, 128).transpose(2, 1, 0).reshape(128, 32)
    cvec[:, 72] = np.tile(f(inp["q_norm_gain"])[0], 2)
    cvec[:, 73] = np.tile(f(inp["k_norm_gain"])[0], 2)
    cvec[:, 74] = f(inp["subln_gain"])[0]
    lamv = np.zeros((128, 256), np.float32)
    for i, k in enumerate(("lambda_q1", "lambda_k1", "lambda_q2", "lambda_k2")):
        lamv[:, i * 64:(i + 1) * 64] = f(inp[k])[0][None, :]
    e1, cst = _consts()
    return {
        "w_in": f(inp["w_in"])[0],
        "w_out": f(inp["w_out"])[0],
        "cvec": cvec,
        "lamv": lamv,
        "relb": f(inp["rel_bias"]),
        "e1": e1,
        "cst": cst,
        "w_rg": np.ascontiguousarray(f(inp["w_rg"])[0].transpose(1, 0, 2)),
        "w_ig": np.ascontiguousarray(f(inp["w_ig"])[0].transpose(1, 0, 2)),
    }


def kernel(**inputs):
    shared = _prep_shared(inputs)
    x = np.asarray(inputs["x"], dtype=np.float32)
    if "nc" not in _NC_CACHE:
        _NC_CACHE["nc"] = build()
    nc = _NC_CACHE["nc"]
    in_maps = []
    for b in range(8):
        m = dict(shared)
        m["x"] = np.ascontiguousarray(x[b])
        in_maps.append(m)
    res = run_bass_kernel_spmd(nc, in_maps, core_ids=list(range(8)))
    return np.stack([np.asarray(r["out"], dtype=np.float32) for r in res.results], axis=0)
```
